# Optimizing a Trainium2 kernel written in Bass

```python
import jax, jax.numpy as jnp
from jax import lax
import numpy as np

D_MODEL = 1024
BATCH = 8
SEQ = 2048
DEPTH = 4
DEC_BATCH = 128
DEC_SEQ = 1
PAST_LEN = 16384
PAGE_SIZE = 128

N_MEM = 256
HEAD_DIM = 64
POOL_WIDTH = D_MODEL // 4
POOL_WINDOWS = (2, 4, 8, 16)
POOL_GROUP = POOL_WIDTH // len(POOL_WINDOWS)
POOL_BUF = max(POOL_WINDOWS) - 1
RWKV_WIDTH = D_MODEL // 2
RWKV_HEADS = RWKV_WIDTH // HEAD_DIM
XA_WIDTH = D_MODEL // 4
XA_HEADS = 4
XA_HEAD_DIM = XA_WIDTH // XA_HEADS
MIX_WIDTH = POOL_WIDTH + RWKV_WIDTH + XA_WIDTH
DECAY_LORA = 64
ICLR_LORA = 64
SHIFT_WIDTH = 3 * RWKV_WIDTH + DECAY_LORA + ICLR_LORA
IN_WIDTH = 2 * POOL_WIDTH + SHIFT_WIDTH + RWKV_WIDTH + 2 * XA_WIDTH
IN_SPLITS = [POOL_WIDTH, 2 * POOL_WIDTH, 2 * POOL_WIDTH + SHIFT_WIDTH,
             2 * POOL_WIDTH + SHIFT_WIDTH + RWKV_WIDTH,
             2 * POOL_WIDTH + SHIFT_WIDTH + RWKV_WIDTH + XA_WIDTH]
EPS = 1e-6
GN_EPS = HEAD_DIM * 1e-5

kernel_name = 'pool_rwkv7_memxattn_hybrid_step'

F32 = jnp.float32


def _rms_norm(x, g):
    xf = x.astype(F32)
    y = xf * lax.rsqrt(jnp.mean(xf * xf, axis=-1, keepdims=True) + EPS)
    return (y * g.astype(F32)).astype(x.dtype)


def _pool_mix(v, buf, start_pos, pool_w, pool_scale):
    B, T, _ = v.shape
    full = jnp.concatenate([buf, v], axis=1).astype(F32)
    csum = jnp.concatenate([jnp.zeros((B, 1, POOL_WIDTH), F32), jnp.cumsum(full, axis=1)], axis=1)
    pos = start_pos + jnp.arange(T)
    cur = full[:, POOL_BUF:]
    hi = csum[:, POOL_BUF + 1:POOL_BUF + 1 + T]
    groups = []
    for gi, win in enumerate(POOL_WINDOWS):
        sl = slice(gi * POOL_GROUP, (gi + 1) * POOL_GROUP)
        lo = csum[:, POOL_BUF + 1 - win:POOL_BUF + 1 - win + T, sl]
        cnt = jnp.minimum(pos + 1, win).astype(F32)[None, :, None]
        groups.append((hi[..., sl] - lo) / cnt - cur[..., sl])
    pooled = jnp.stack(groups, axis=2)
    y = jnp.einsum('btng,nge->btne', pooled, pool_w.astype(F32)).reshape(B, T, POOL_WIDTH)
    return (y * pool_scale.astype(F32)).astype(v.dtype)


def _rwkv7(xs, S0, w0, w_w2, a0, w_a2, k_k, k_a, r_k, ln_g, ln_b):
    B, T, _ = xs.shape
    H, N, R = RWKV_HEADS, HEAD_DIM, RWKV_WIDTH
    xs = xs.astype(F32)
    r = xs[..., :R]
    k = xs[..., R:2 * R]
    v = xs[..., 2 * R:3 * R]
    wd = xs[..., 3 * R:3 * R + DECAY_LORA]
    ad = xs[..., 3 * R + DECAY_LORA:]
    w = -jax.nn.softplus(-(w0.astype(F32) + jnp.tanh(wd) @ w_w2.astype(F32))) - 0.5
    decay = jnp.exp(-jnp.exp(w))
    a = jax.nn.sigmoid(a0.astype(F32) + ad @ w_a2.astype(F32))
    heads = lambda t: t.reshape(B, T, H, N)
    kk = heads(k * k_k.astype(F32))
    kk = kk / jnp.maximum(jnp.sqrt(jnp.sum(kk * kk, axis=-1, keepdims=True)), 1e-12)
    k = k * (1.0 + (a - 1.0) * k_a.astype(F32))
    r, k, v, decay, a = heads(r), heads(k), heads(v), heads(decay), heads(a)

    def step(S, inp):
        r_t, w_t, k_t, v_t, kk_t, a_t = inp
        s_kk = jnp.einsum('bhij,bhj->bhi', S, kk_t)
        S = (S * w_t[:, :, None, :] - s_kk[..., None] * (kk_t * a_t)[:, :, None, :]
             + v_t[..., None] * k_t[:, :, None, :])
        return S, jnp.einsum('bhij,bhj->bhi', S, r_t)

    tm = lambda t: jnp.swapaxes(t, 0, 1)
    S_T, y = lax.scan(step, S0.astype(F32), (tm(r), tm(decay), tm(k), tm(v), tm(kk), tm(a)))
    y = tm(y)
    mu = jnp.mean(y, axis=-1, keepdims=True)
    var = jnp.mean(jnp.square(y - mu), axis=-1, keepdims=True)
    y = (y - mu) * lax.rsqrt(var + GN_EPS) * ln_g.astype(F32).reshape(H, N) + ln_b.astype(F32).reshape(H, N)
    y = y + jnp.sum(r * k * r_k.astype(F32), axis=-1, keepdims=True) * v
    return y.reshape(B, T, R), S_T


def _mem_kv(mem, g, w_kv):
    B = mem.shape[0]
    kv = _rms_norm(mem, g) @ w_kv
    k = kv[..., :XA_WIDTH].reshape(B, N_MEM, XA_HEADS, XA_HEAD_DIM)
    v = kv[..., XA_WIDTH:].reshape(B, N_MEM, XA_HEADS, XA_HEAD_DIM)
    return k, v


def _cross_attend(q, mk, mv):
    B, T, _ = q.shape
    qh = q.reshape(B, T, XA_HEADS, XA_HEAD_DIM).astype(F32)
    s = jnp.einsum('bthd,bmhd->bhtm', qh, mk.astype(F32)) * (XA_HEAD_DIM ** -0.5)
    p = jax.nn.softmax(s, axis=-1)
    o = jnp.einsum('bhtm,bmhd->bthd', p, mv.astype(F32))
    return o.reshape(B, T, XA_WIDTH).astype(q.dtype)


def _trunk(x, start_pos, pool_buf, shift_prev, wkv, mem_k, mem_v, p):
    new_pool, new_shift, new_wkv = [], [], []
    for l in range(DEPTH):
        xn = _rms_norm(x, p['norm_g'][l])
        h = xn @ p['w_in'][l]
        pool_v, pool_g, rw, rw_g, q, xa_g = jnp.split(h, IN_SPLITS, axis=-1)
        pool_y = _pool_mix(pool_v, pool_buf[l], start_pos, p['pool_w'][l], p['pool_scale'][l]) * jax.nn.silu(pool_g)
        new_pool.append(jnp.concatenate([pool_buf[l].astype(pool_v.dtype), pool_v], axis=1)[:, -POOL_BUF:])
        prev = jnp.concatenate([shift_prev[l][:, None].astype(rw.dtype), rw[:, :-1]], axis=1)
        rws = rw + (prev - rw) * p['shift_mu'][l]
        new_shift.append(rw[:, -1])
        rwkv_y, S_T = _rwkv7(rws, wkv[l], p['w0'][l], p['w_w2'][l], p['a0'][l], p['w_a2'][l],
                             p['k_k'][l], p['k_a'][l], p['r_k'][l], p['ln_x_g'][l], p['ln_x_b'][l])
        new_wkv.append(S_T.astype(x.dtype))
        rwkv_y = rwkv_y.astype(x.dtype) * jax.nn.silu(rw_g)
        xa_y = _cross_attend(q, mem_k[l], mem_v[l]) * jax.nn.silu(xa_g)
        mixed = jnp.concatenate([pool_y, rwkv_y, xa_y], axis=-1)
        x = x + mixed @ p['w_out'][l]
    y = _rms_norm(x, p['final_norm_g'])
    return y, jnp.stack(new_pool), jnp.stack(new_shift), jnp.stack(new_wkv)


def setup_inputs(seed: int = 0) -> dict:
    key = jax.random.key(seed)
    ks = jax.random.split(key, 32)
    nrm = lambda k, shape, s: jax.random.normal(k, shape, F32) * s
    L = DEPTH
    return {
        'x_prompt': nrm(ks[0], (BATCH, SEQ, D_MODEL), 1.0),
        'x_sample': nrm(ks[1], (DEC_BATCH, DEC_SEQ, D_MODEL), 1.0),
        'mem_prompt': nrm(ks[2], (BATCH, N_MEM, D_MODEL), 1.0),
        'state_pool': nrm(ks[3], (L, DEC_BATCH, POOL_BUF, POOL_WIDTH), 1.0),
        'state_shift': nrm(ks[4], (L, DEC_BATCH, SHIFT_WIDTH), 1.0),
        'state_wkv': nrm(ks[5], (L, DEC_BATCH, RWKV_HEADS, HEAD_DIM, HEAD_DIM), 0.5),
        'cache_mem_k': nrm(ks[6], (L, DEC_BATCH, N_MEM, XA_HEADS, XA_HEAD_DIM), 1.0),
        'cache_mem_v': nrm(ks[7], (L, DEC_BATCH, N_MEM, XA_HEADS, XA_HEAD_DIM), 1.0),
        'norm_g': 1.0 + nrm(ks[8], (L, D_MODEL), 0.05),
        'w_in': nrm(ks[9], (L, D_MODEL, IN_WIDTH), D_MODEL ** -0.5),
        'w_out': nrm(ks[10], (L, MIX_WIDTH, D_MODEL), 0.5 * MIX_WIDTH ** -0.5),
        'pool_w': nrm(ks[11], (L, len(POOL_WINDOWS), POOL_GROUP, POOL_GROUP), POOL_GROUP ** -0.5),
        'pool_scale': 1.0 + nrm(ks[12], (L, POOL_WIDTH), 0.1),
        'shift_mu': jax.random.uniform(ks[13], (L, SHIFT_WIDTH), F32),
        'w0': -1.0 + nrm(ks[14], (L, RWKV_WIDTH), 0.5),
        'w_w2': nrm(ks[15], (L, DECAY_LORA, RWKV_WIDTH), 0.1),
        'a0': nrm(ks[16], (L, RWKV_WIDTH), 0.1),
        'w_a2': nrm(ks[17], (L, ICLR_LORA, RWKV_WIDTH), 0.1),
        'k_k': 0.85 + nrm(ks[18], (L, RWKV_WIDTH), 0.05),
        'k_a': 1.0 + nrm(ks[19], (L, RWKV_WIDTH), 0.05),
        'r_k': nrm(ks[20], (L, RWKV_HEADS, HEAD_DIM), 0.1),
        'ln_x_g': 1.0 + nrm(ks[21], (L, RWKV_WIDTH), 0.1),
        'ln_x_b': nrm(ks[22], (L, RWKV_WIDTH), 0.01),
        'mem_norm_g': 1.0 + nrm(ks[23], (L, D_MODEL), 0.05),
        'w_kv': nrm(ks[24], (L, D_MODEL, 2 * XA_WIDTH), D_MODEL ** -0.5),
        'final_norm_g': 1.0 + nrm(ks[25], (D_MODEL,), 0.05),
    }


def reference(x_prompt, x_sample, mem_prompt, state_pool, state_shift, state_wkv, cache_mem_k, cache_mem_v,
              norm_g, w_in, w_out, pool_w, pool_scale, shift_mu, w0, w_w2, a0, w_a2, k_k, k_a, r_k,
              ln_x_g, ln_x_b, mem_norm_g, w_kv, final_norm_g):
    params = {'norm_g': norm_g, 'w_in': w_in, 'w_out': w_out, 'pool_w': pool_w, 'pool_scale': pool_scale,
              'shift_mu': shift_mu, 'w0': w0, 'w_w2': w_w2, 'a0': a0, 'w_a2': w_a2, 'k_k': k_k, 'k_a': k_a,
              'r_k': r_k, 'ln_x_g': ln_x_g, 'ln_x_b': ln_x_b, 'final_norm_g': final_norm_g}
    mk_list, mv_list = [], []
    for l in range(DEPTH):
        mk, mv = _mem_kv(mem_prompt, mem_norm_g[l], w_kv[l])
        mk_list.append(mk)
        mv_list.append(mv)
    memk_prompt = jnp.stack(mk_list)
    memv_prompt = jnp.stack(mv_list)
    dt = x_prompt.dtype
    pool0 = jnp.zeros((DEPTH, BATCH, POOL_BUF, POOL_WIDTH), dt)
    shift0 = jnp.zeros((DEPTH, BATCH, SHIFT_WIDTH), dt)
    wkv0 = jnp.zeros((DEPTH, BATCH, RWKV_HEADS, HEAD_DIM, HEAD_DIM), F32)
    y_prompt, pool_prompt, shift_prompt, wkv_prompt = _trunk(
        x_prompt, 0, pool0, shift0, wkv0, memk_prompt, memv_prompt, params)
    y_sample, pool_sample, shift_sample, wkv_sample = _trunk(
        x_sample, PAST_LEN, state_pool, state_shift, state_wkv, cache_mem_k, cache_mem_v, params)
    return (y_prompt, y_sample, pool_prompt, shift_prompt, wkv_prompt, memk_prompt, memv_prompt,
            pool_sample, shift_sample, wkv_sample)
```

```python
import contextlib
import numpy as np
import concourse.bass as bass
import concourse.mybir as mybir
from concourse.bass_utils import run_bass_kernel_spmd

F32 = mybir.dt.float32
BF16 = mybir.dt.bfloat16
AF = mybir.ActivationFunctionType
ALU = mybir.AluOpType
AX = mybir.AxisListType

L_ALL = 4
D = 1024
T = 2048
NS = 16
NT = 512
EPS = 1e-6
GN_EPS = 64 * 1e-5
LWC = 0.6065306597126334
VL = 76
NCST = 1122
NCSTB = 640
SEGW = 1040
G_POOL, G_XA, G_LORA, G_RW = 0, 4, 8, 9
FC_ORDER = [0, 1, 2, 3, 21, 22, 23, 24, 16] + [x for c in range(4) for x in (4 + c, 8 + c, 12 + c, 17 + c)]


DRAM_NAMES = set()
PSUM_NAMES = set()


def _dsize(dt):
    return 2 if dt == BF16 else 4


class Op:
    __slots__ = ("eng", "fn", "deps", "signal", "sigval", "sem", "lane", "id", "row")


class Prog:
    ENGS = ["pe", "act", "dve", "pool", "sp"]

    def __init__(self):
        self.ops = []
        self.recs = {}
        self.lanes = {}
        self.nlane = {"sp": 8, "pool": 8, "act": 4}
        self.rr = {"sp": 0, "pool": 0, "act": 0}

    @staticmethod
    def region(ap):
        a = ap.ap
        pitch, pcnt = a[0]
        off = ap.offset
        ds = _dsize(ap.dtype)
        if ap.name in DRAM_NAMES:
            lo = off
            hi = off + sum(abs(s) * (c - 1) for s, c in a) + 1
            return (ap.name, 0, 1, lo * ds, hi * ds)
        p0 = off // pitch if pitch else 0
        c0 = off - p0 * pitch if pitch else off
        ext = sum(abs(s) * (c - 1) for s, c in a[1:]) + 1
        if ap.name in PSUM_NAMES:
            b0 = (c0 * ds) // 2048 * 2048
            b1 = -((-(c0 + ext) * ds) // 2048) * 2048
            return (ap.name, 0, 128, b0, b1)
        return (ap.name, p0, p0 + pcnt, c0 * ds, (c0 + ext) * ds)

    def add(self, eng, fn, R=(), W=(), dma=False, row=None):
        op = Op()
        op.row = row
        op.eng, op.fn, op.signal, op.sigval, op.sem, op.lane = eng, fn, False, 0, None, None
        op.id = len(self.ops)
        deps = set()
        if dma:
            k = self.rr[eng]
            self.rr[eng] = (k + 1) % self.nlane[eng]
            op.lane = "%s%d" % (eng, k)
            lst = self.lanes.setdefault(op.lane, [])
            if lst:
                deps.add(lst[-1])
            lst.append(op)
        for ap, isw in [(x, False) for x in R] + [(x, True) for x in W]:
            name, p0, p1, b0, b1 = self.region(ap)
            isps = name in PSUM_NAMES
            lst = self.recs.setdefault(name, [])
            keep = []
            for rec in lst:
                q0, q1, c0, c1, o, w = rec
                ov = not (q1 <= p0 or p1 <= q0 or c1 <= b0 or b1 <= c0)
                if ov and isps and (o.eng != eng or (eng == "pe" and o.row != row)):
                    deps.add(o)
                elif ov and (isw or w):
                    same = (o.eng == eng and o.lane is None and not dma)
                    if not (same and eng == "pe"):
                        deps.add(o)
                    elif same and eng != "pe" and w:
                        deps.add(o)
                cov = (isw or isps) and (p0 <= q0 and q1 <= p1 and b0 <= c0 and c1 <= b1)
                dup = (not isw) and (not w) and o.eng == eng and o.lane is None and not dma and (q0, q1, c0, c1) == (p0, p1, b0, b1)
                if not (cov or dup):
                    keep.append(rec)
            keep.append([p0, p1, b0, b1, op, isw])
            self.recs[name] = keep
        deps.discard(op)
        best = {}
        for d in list(deps):
            if d.lane is None:
                if d.eng not in best or best[d.eng].id < d.id:
                    best[d.eng] = d
        deps = set(d for d in deps if d.lane is not None) | set(best.values())
        op.deps = deps
        for d in deps:
            d.signal = True
        self.ops.append(op)
        return op

    def emit(self, nc, block, sems):
        cnt = {}
        for op in self.ops:
            if op.lane is not None:
                op.sem = sems[op.lane]
                cnt[op.lane] = cnt.get(op.lane, 0) + 16
                op.sigval = cnt[op.lane]
            elif op.signal:
                op.sem = sems[op.eng]
                cnt[op.eng] = cnt.get(op.eng, 0) + 1
                op.sigval = cnt[op.eng]
        per = {e: [o for o in self.ops if o.eng == e] for e in self.ENGS}

        def run(eng_name, e):
            known = {}
            for op in per[eng_name]:
                need = {}
                for d in op.deps:
                    if d.sem is None:
                        continue
                    key = id(d.sem)
                    if need.get(key, (None, 0))[1] < d.sigval:
                        need[key] = (d.sem, d.sigval)
                for key, (sem, val) in need.items():
                    if known.get(key, 0) < val:
                        e.wait_ge(sem, val)
                        known[key] = val
                ins = op.fn(e)
                if ins is not None and op.sem is not None:
                    ins.then_inc(op.sem, 16 if op.lane is not None else 1)

        block.tensor(lambda e: run("pe", e))
        block.scalar(lambda e: run("act", e))
        block.vector(lambda e: run("dve", e))
        block.gpsimd(lambda e: run("pool", e))
        block.sync(lambda e: run("sp", e))


def build_program(NL=L_ALL):
    nc = bass.Bass("TRN2", target_bir_lowering=False)
    P = Prog()

    def din(name, shape):
        DRAM_NAMES.add(name)
        return nc.dram_tensor(name, list(shape), F32, kind="ExternalInput").ap()

    def dout(name, shape):
        DRAM_NAMES.add(name)
        return nc.dram_tensor(name, list(shape), F32, kind="ExternalOutput").ap()

    xT = din("xT", [D, T]); xsT = din("xsT", [D, NS]); memT = din("memT", [D, 256])
    spT = din("spT", [L_ALL, 256, NS, 15]); sshT = din("sshT", [L_ALL, 1664, NS]); swkv = din("swkv", [L_ALL, 128, 4096])
    ckT = din("ckT", [L_ALL, NS, 2, 128, 256]); cv = din("cv", [L_ALL, NS, 256, 256])
    win = din("win", [L_ALL, 25, 128, 8, 128]); wout = din("wout", [L_ALL, 8, 128, 8, 128]); wkv = din("wkv", [L_ALL, 128, 8, 512])
    vec = din("vec", [128, VL * L_ALL + 8]); lora2 = din("lora2", [L_ALL, 128, 512]); pwb = din("pwb", [L_ALL, 2, 128, 128])
    cst = din("cst", [128, NCST]); cstb = din("cstb", [128, NCSTB]); bhv = din("bhv", [L_ALL, 128, 192])
    yT = dout("yT", [D, T]); ysT = dout("ysT", [D, NS]); poolp = dout("poolp", [L_ALL, 256, 15]); shiftp = dout("shiftp", [L_ALL, 128, 13])
    wkvp = dout("wkvp", [L_ALL, 4, 128, 64]); memk = dout("memk", [L_ALL, 2, 128, 256]); memv = dout("memv", [L_ALL, 256, 256])
    pools = dout("pools", [L_ALL, 256, NS, 15]); shifts = dout("shifts", [L_ALL, 128, 13, NS]); wkvs = dout("wkvs", [L_ALL, 128, 4096])
    DRAM_NAMES.update(["scr6", "scry"])
    scr6 = nc.dram_tensor("scr6", [NS, 8, 6, 64], F32, kind="Internal").ap()
    scry = nc.dram_tensor("scry", [NS, 512], F32, kind="Internal").ap()

    es = contextlib.ExitStack()
    with es:
        def sb(name, shape, dt):
            return es.enter_context(nc.sbuf_tensor(name, list(shape), dt))

        def ps(name, shape, dt):
            PSUM_NAMES.add(name)
            return es.enter_context(nc.psum_tensor(name, list(shape), dt))

        X = sb("X", [128, 8, T + NS], F32)
        XN = sb("XN", [128, 8, SEGW], BF16)
        MIX = sb("MIX", [128, 8, SEGW], BF16)
        LTS = sb("LTS", [128, SEGW], BF16)
        VEC = sb("VEC", [128, VL * L_ALL + 8], F32)
        CST = sb("CST", [128, NCST], F32)
        CSTB = sb("CSTB", [128, NCSTB], BF16)
        LORA2 = sb("LORA2", [128, 512], BF16)
        PWB = sb("PWB", [128, 2, 128], BF16)
        KTm = sb("KTm", [128, 2, 256], BF16)
        VTm = sb("VTm", [128, 2, 256], BF16)
        ST = sb("ST", [128, 4, 64], F32)
        STb = sb("STb", [128, 4, 64], BF16)
        HALO = sb("HALO", [128, 2, 15], F32)
        PREV = sb("PREV", [128, 13], F32)
        WINB = [sb("WINB%d" % i, [128, 4, 8, 128], BF16) for i in range(2)]
        WOB = [sb("WOB%d" % i, [128, 8, 128], BF16) for i in range(2)]
        NF = 9216
        NB = 17408
        SF = sb("SF", [128, NF], F32)
        SBF = sb("SBF", [128, NB], BF16)
        psA = ps("psA", [128, 1024], F32); psB = ps("psB", [128, 1024], F32)
        psC = ps("psC", [128, 512], F32); psD = ps("psD", [128, 512], F32); psE = ps("psE", [128, 512], F32)
        psT = ps("psT", [128, 1024], BF16)

        class Carve:
            def __init__(s, t, n):
                s.t, s.n, s.o = t, n, 0

            def get(s, cols):
                a = s.t[:, s.o:s.o + cols]
                s.o += cols
                assert s.o <= s.n, (s.o, s.n)
                return a

            def reset(s):
                s.o = 0
        cf = Carve(SF, NF); cb = Carve(SBF, NB)

        ID32 = CST[:, 0:128]; MSU = CST[:, 128:192]; MUU = CST[:, 192:256]; MSL = CST[:, 256:320]
        EYE16 = CST[:, 320:576]; RWIN = CST[:, 576:578]; RCF = CST[:, 578:610]; SEGM = CST[:, 610:1122]
        IDB = CSTB[:, 0:128]; ONESMEAN = CSTB[:, 128:256]; BONES = CSTB[:, 256:384]; BMEAN = CSTB[:, 384:512]
        ID64 = CSTB[:, 512:576]; ONES64 = CSTB[:, 576:640]

        def V(l, off, n=1):
            return VEC[:, l * VL + off: l * VL + off + n]
        O_G, O_MU, O_OMU, O_W0, O_A0, O_KK, O_KA, O_OMKA, O_RK, O_LNG, O_LNB, O_PS, O_MG = 0, 8, 21, 34, 38, 42, 46, 50, 54, 58, 62, 66, 68
        FG = VEC[:, VL * L_ALL: VL * L_ALL + 8]

        def dma(q, out, in_):
            P.add(q, lambda e: e.dma_start(out=out, in_=in_), R=[in_], W=[out], dma=True)

        def mm(out, lhsT, rhs, start=True, stop=True, tp=None, sgc=False):
            kw = {"skip_group_check": True} if sgc else {}
            if tp is None:
                P.add("pe", lambda e: e.matmul(out, lhsT, rhs, start=start, stop=stop, **kw), R=[lhsT, rhs], W=[out], row=-1)
            else:
                P.add("pe", lambda e: e.matmul(out, lhsT, rhs, start=start, stop=stop, tile_position=tp, **kw), R=[lhsT, rhs], W=[out], row=tp[0])

        def tr(out, in_, ident):
            P.add("pe", lambda e: e.transpose(out, in_, ident), R=[in_, ident], W=[out], row=-1)

        def act(out, in_, func, bias=None, scale=None):
            kw = {}
            R = [in_]
            if bias is not None:
                kw["bias"] = bias
                if not isinstance(bias, float):
                    R.append(bias)
            if scale is not None:
                kw["scale"] = scale
                if not isinstance(scale, float):
                    R.append(scale)
            P.add("act", lambda e: e.activation(out=out, in_=in_, func=func, **kw), R=R, W=[out])

        def tt(eng, out, a, b, op):
            P.add(eng, lambda e: e.tensor_tensor(out=out, in0=a, in1=b, op=op), R=[a, b], W=[out])

        def ts(eng, out, a, s1, op0, s2=None, op1=None):
            R = [a] + [s for s in (s1, s2) if s is not None and not isinstance(s, float)]
            if op1 is None:
                P.add(eng, lambda e: e.tensor_scalar(out=out, in0=a, scalar1=s1, scalar2=None, op0=op0), R=R, W=[out])
            else:
                P.add(eng, lambda e: e.tensor_scalar(out=out, in0=a, scalar1=s1, scalar2=s2, op0=op0, op1=op1), R=R, W=[out])

        def stt(eng, out, a, s, b, op0, op1):
            eng = "dve"
            R = [a, b] + ([] if isinstance(s, float) else [s])
            P.add(eng, lambda e: e.scalar_tensor_tensor(out=out, in0=a, scalar=s, in1=b, op0=op0, op1=op1), R=R, W=[out])

        def cp(eng, out, in_):
            if eng == "act":
                act(out, in_, AF.Copy)
            else:
                P.add(eng, lambda e: e.tensor_copy(out=out, in_=in_), R=[in_], W=[out])

        def recip(out, in_):
            P.add("dve", lambda e: e.reciprocal(out=out, in_=in_), R=[in_], W=[out])

        def rsum(out, in_):
            P.add("dve", lambda e: e.reduce_sum(out=out, in_=in_, axis=AX.X), R=[in_], W=[out])

        def rmax(out, in_):
            P.add("dve", lambda e: e.reduce_max(out=out, in_=in_, axis=AX.X), R=[in_], W=[out])

        def memset(eng, out, val):
            P.add(eng, lambda e: e.memset(out, val), W=[out])

        ew_rr = [0]

        def EW():
            ew_rr[0] ^= 1
            return "dve" if ew_rr[0] else "pool"

        dma("sp", VEC[:], vec); dma("sp", CST[:], cst); dma("pool", CSTB[:], cstb)
        for kc in range(8):
            dma("sp", X[:, kc, 0:T], xT[kc * 128:(kc + 1) * 128, :])
        dma("sp", X[:, :, T:T + NS], xsT.rearrange("(k p) b -> p k b", p=128))
        for l in range(NL):
            ts("dve", V(l, O_OMU, 13), V(l, O_MU, 13), -1.0, ALU.mult, 1.0, ALU.add)
            ts("dve", V(l, O_OMKA, 4), V(l, O_KA, 4), -1.0, ALU.mult, 1.0, ALU.add)

        def rms_rstd(srcf, W, rs_out, sq_tmp, pst):
            for kc in range(8):
                sq = sq_tmp[kc % 2]
                act(sq[:, :W], srcf(kc), AF.Square)
                mm(pst[:, :W], ONESMEAN, sq[:, :W], start=(kc == 0), stop=(kc == 7))
            act(rs_out, pst[:, :W], AF.Sqrt, bias=EPSC[:, 0:1])
            recip(rs_out, rs_out)

        EPSC = sb("EPSC", [128, 4], F32)
        memset("dve", EPSC[:, 0:1], EPS); memset("dve", EPSC[:, 1:2], GN_EPS); memset("dve", EPSC[:, 2:3], 0.0)

        def mem_stage(l):
            cf.reset(); cb.reset()
            MEMX = cf.get(8 * 256).rearrange("p (k m) -> p k m", k=8)
            RSM = cf.get(256); KO = cf.get(256); VO = cf.get(256)
            SQT = [cb.get(512), cb.get(512)]
            MN = cb.get(8 * 256).rearrange("p (k m) -> p k m", k=8)
            WKVB = cb.get(8 * 512).rearrange("p (k f) -> p k f", k=8)
            dma("sp", MEMX, memT.rearrange("(k p) m -> p k m", p=128))
            dma("pool", WKVB, wkv[l])
            rms_rstd(lambda kc: MEMX[:, kc, :], 256, RSM, SQT, psC)
            for kc in range(8):
                stt("dve", MN[:, kc, :], MEMX[:, kc, :], V(l, O_MG + kc), RSM, ALU.mult, ALU.mult)
            for fc in range(2):
                for kc in range(8):
                    mm(psA[:, fc * 512: fc * 512 + 256], WKVB[:, kc, fc * 128:(fc + 1) * 128], MN[:, kc, :], start=(kc == 0), stop=(kc == 7))
                cp("act", KTm[:, fc, :], psA[:, fc * 512: fc * 512 + 256])
                cp("dve", KO, psA[:, fc * 512: fc * 512 + 256])
                dma("sp", memk[l, fc], KO)
            for mt in range(2):
                for kc in range(8):
                    mm(psB[:, mt * 512: mt * 512 + 256], MN[:, kc, mt * 128:(mt + 1) * 128], WKVB[:, kc, 256:512], start=(kc == 0), stop=(kc == 7))
                cp("act", VTm[:, mt, :], psB[:, mt * 512: mt * 512 + 256])
                cp("dve", VO, psB[:, mt * 512: mt * 512 + 256])
                dma("sp", memv[l, mt * 128:(mt + 1) * 128, :], VO)

        P.marks = []

        def mark(nm):
            P.marks.append((nm, len(P.ops)))
        mark("mem_done")
        SEGS = [[("P", 0), ("P", 1)], [("P", 2), ("P", 3), ("S", 0)]]

        def tiles_of(seg):
            out = []
            lc = 0
            for kind, i in seg:
                W = NT if kind == "P" else NS
                gx = i * NT if kind == "P" else T
                out.append(dict(kind=kind, i=i, W=W, gx=gx, lc=lc))
                lc += W
            return out

        wq = [0]

        GROUPS = [("POOL", G_POOL, 4), ("XA", G_XA, 4), ("LORA", G_LORA, 1)] + [("RW%d" % c_, G_RW + 4 * c_, 4) for c_ in range(4)]
        plan = [(l_, s_, g_) for l_ in range(NL) for s_ in range(2) for g_ in range(len(GROUPS))]
        loaded = {}

        def get_w(l_, s_, g_):
            i0 = plan.index((l_, s_, g_))
            for i in (i0, i0 + 1):
                if i < len(plan) and plan[i] not in loaded:
                    pl, ps_, pg = plan[i]
                    loaded[plan[i]] = load_win(pl, GROUPS[pg][1], GROUPS[pg][2])
            return loaded[(l_, s_, g_)]

        def load_win(l, pos, n):
            wb = WINB[wq[0] % 2]
            wq[0] += 1
            dma("pool", wb[:, 0:n], win[l, pos:pos + n].rearrange("c p k f -> p c k f"))
            return wb

        def win_mm(wb, i, tl, out):
            W, lc = tl["W"], tl["lc"]
            for kc in range(8):
                mm(out[:, :W], wb[:, i, kc, :], XN[:, kc, lc:lc + W], start=(kc == 0), stop=(kc == 7))

        def tshift(l, j, pst, tl, xs, a1, SSH=None, SHS=None):
            W = tl["W"]
            act(a1[:, :W], pst[:, :W], AF.Identity, scale=V(l, O_OMU + j))
            if tl["kind"] == "P":
                stt("dve", xs[:, 1:W], pst[:, 0:W - 1], V(l, O_MU + j), a1[:, 1:W], ALU.mult, ALU.add)
                stt("dve", xs[:, 0:1], PREV[:, j:j + 1], V(l, O_MU + j), a1[:, 0:1], ALU.mult, ALU.add)
                cp("act", PREV[:, j:j + 1], pst[:, W - 1:W])
            else:
                stt("dve", xs[:, :W], SSH[:, j, :], V(l, O_MU + j), a1[:, :W], ALU.mult, ALU.add)
                cp("act", SHS[:, j, :], pst[:, :W])

        def norm_gen(l_, tls_):
            cf.reset(); cb.reset()
            SQT_ = [cb.get(512), cb.get(512)]
            RS_ = cf.get(SEGW)
            for tl_ in tls_:
                W, gx, lc = tl_["W"], tl_["gx"], tl_["lc"]
                rms_rstd(lambda kc: X[:, kc, gx:gx + W], W, RS_[:, lc:lc + W], SQT_, psC)
                yield
                for kc in range(8):
                    stt("dve", XN[:, kc, lc:lc + W], X[:, kc, gx:gx + W], V(l_, O_G + kc), RS_[:, lc:lc + W], ALU.mult, ALU.mult)
                    if kc % 2 == 1:
                        yield

        for l in range(NL):
            dma("pool", LORA2[:], lora2[l]); dma("pool", PWB[:], pwb[l].rearrange("c p f -> p c f"))
            mem_stage(l)
            memset("dve", HALO[:], 0.0); memset("dve", PREV[:], 0.0); memset("dve", ST[:], 0.0); memset("dve", STb[:], 0.0)
            for si, seg in enumerate(SEGS):
                tls = tiles_of(seg)
                has_s = any(t["kind"] == "S" for t in tls)
                dma("pool", WOB[0][:], wout[l, 0]); dma("pool", WOB[1][:], wout[l, 1])
                if l == 0 and si == 0:
                    for _ in norm_gen(l, tls):
                        pass
                mark("norm_done")
                cf.reset(); cb.reset()
                wb = get_w(l, si, 0)
                PVB = cf.get(2 * 527).rearrange("p (c t) -> p c t", c=2)
                SGP = cf.get(1024).rearrange("p (c t) -> p c t", c=2)
                W2 = cf.get(2 * 526).rearrange("p (c t) -> p c t", c=2)
                W4 = cf.get(2 * 524).rearrange("p (c t) -> p c t", c=2)
                W8 = cf.get(520); W16 = cf.get(512)
                TMPP = cf.get(32).rearrange("p (c t) -> p c t", c=2)
                PL = cb.get(1024).rearrange("p (c t) -> p c t", c=2)
                SP = cf.get(2 * 16 * 16).rearrange("p (c b r) -> p c b r", c=2, b=16)
                W2s = cf.get(2 * 16 * 15).rearrange("p (c b r) -> p c b r", c=2, b=16)
                W4s = cf.get(2 * 16 * 13).rearrange("p (c b r) -> p c b r", c=2, b=16)
                W8s = cf.get(16 * 9).rearrange("p (b r) -> p b r", b=16)
                W16s = cf.get(16).rearrange("p (b r) -> p b r", b=16)
                for tl in tls:
                    W, gx, lc = tl["W"], tl["gx"], tl["lc"]
                    outs = [psA[:, 0:512], psA[:, 512:1024], psB[:, 0:512], psB[:, 512:1024]]
                    for i in range(4):
                        win_mm(wb, i, tl, outs[i])
                    if tl["kind"] == "P":
                        cp("pool", PVB[:, :, 0:15], HALO[:])
                        for ch in range(2):
                            cp("act", PVB[:, ch, 15:15 + W], outs[ch][:, :W])
                            act(SGP[:, ch, :W], outs[2 + ch][:, :W], AF.Silu)
                        cp("pool", HALO[:], PVB[:, :, W:W + 15])
                        if tl["i"] == 3:
                            dma("sp", poolp[l].rearrange("(c p) r -> p c r", p=128), PVB[:, :, W:W + 15])
                        tt("dve", W2[:, :, 0:W + 14], PVB[:, :, 1:W + 15], PVB[:, :, 0:W + 14], ALU.add)
                        tt("pool", W4[:, :, 0:W + 12], W2[:, :, 2:W + 14], W2[:, :, 0:W + 12], ALU.add)
                        tt("dve", W8[:, 0:W + 8], W4[:, 1, 4:W + 12], W4[:, 1, 0:W + 8], ALU.add)
                        tt("pool", W16[64:128, 0:W], W8[64:128, 8:W + 8], W8[64:128, 0:W], ALU.add)
                        wsum = [(0, 0, 64, W2[0:64, 0, 14:14 + W]), (0, 64, 128, W4[64:128, 0, 12:12 + W]),
                                (1, 0, 64, W8[0:64, 8:8 + W]), (1, 64, 128, W16[64:128, 0:W])]
                        for ch, p0, p1, ws in wsum:
                            stt("dve", PL[p0:p1, ch, :W], ws, RWIN[p0:p1, ch:ch + 1], PVB[p0:p1, ch, 15:15 + W], ALU.mult, ALU.subtract)
                            if tl["i"] == 0:
                                tt("dve", TMPP[p0:p1, ch, 0:15], ws[:, 0:15], RCF[p0:p1, ch * 16:ch * 16 + 15], ALU.mult)
                                tt("dve", PL[p0:p1, ch, 0:15], TMPP[p0:p1, ch, 0:15], PVB[p0:p1, ch, 15:30], ALU.subtract)
                        pl = [PL[:, 0, :W], PL[:, 1, :W]]
                        sg = [SGP[:, 0, :W], SGP[:, 1, :W]]
                    else:
                        for ch in range(2):
                            dma("sp", SP[:, ch, :, 0:15], spT[l][ch * 128:(ch + 1) * 128])
                        for ch in range(2):
                            cp("act", SP[:, ch, :, 15], outs[ch][:, :W])
                            act(SGP[:, ch, :W], outs[2 + ch][:, :W], AF.Silu)
                        for ch in range(2):
                            dma("sp", pools[l][ch * 128:(ch + 1) * 128], SP[:, ch, :, 1:16])
                        tt("dve", W2s[:], SP[:, :, :, 1:16], SP[:, :, :, 0:15], ALU.add)
                        tt("dve", W4s[:], W2s[:, :, :, 2:15], W2s[:, :, :, 0:13], ALU.add)
                        tt("dve", W8s[:], W4s[:, 1, :, 4:13], W4s[:, 1, :, 0:9], ALU.add)
                        tt("dve", W16s[64:128], W8s[64:128, :, 8:9], W8s[64:128, :, 0:1], ALU.add)
                        wsum = [(0, 0, 64, W2s[0:64, 0, :, 14]), (0, 64, 128, W4s[64:128, 0, :, 12]),
                                (1, 0, 64, W8s[0:64, :, 8]), (1, 64, 128, W16s[64:128, :, 0])]
                        for ch, p0, p1, ws in wsum:
                            stt("dve", PL[p0:p1, ch, :W], ws, RWIN[p0:p1, ch:ch + 1], SP[p0:p1, ch, :, 15], ALU.mult, ALU.subtract)
                        pl = [PL[:, 0, :W], PL[:, 1, :W]]
                        sg = [SGP[:, 0, :W], SGP[:, 1, :W]]
                    for ch in range(2):
                        pso = psC if ch == 0 else psD
                        mm(pso[:, :W], PWB[:, ch, :], pl[ch])
                        stt("dve", MIX[:, ch, lc:lc + W], pso[:, :W], V(l, O_PS + ch), sg[ch], ALU.mult, ALU.mult)
                mark("pool_done")
                cf.reset(); cb.reset()
                wb = get_w(l, si, 1)
                QB = cb.get(1024).rearrange("p (c t) -> p c t", c=2)
                ET2 = [cb.get(1024).rearrange("p (c t) -> p c t", c=2), cb.get(1024).rearrange("p (c t) -> p c t", c=2)]
                KC2 = [cb.get(4 * 512).rearrange("p (b c m) -> p b c m", b=4, c=2) for _p in range(2)]
                VC2 = [cb.get(4 * 512).rearrange("p (b t f) -> p b t f", b=4, t=2) for _p in range(2)]

                def loadK(bg):
                    for bb in range(4):
                        dma("pool", KC2[bg % 2][:, bb], ckT[l, bg * 4 + bb].rearrange("c p m -> p c m"))

                def loadV(bg):
                    for bb in range(4):
                        dma("pool", VC2[bg % 2][:, bb], cv[l, bg * 4 + bb].rearrange("(t p) f -> p t f", p=128))
                if has_s:
                    loadK(0); loadV(0)
                SGX = cf.get(1024).rearrange("p (c t) -> p c t", c=2)
                RD = cf.get(512); TO = cf.get(512)
                for tl in tls:
                    W, gx, lc = tl["W"], tl["gx"], tl["lc"]
                    outs = [psA[:, 0:512], psA[:, 512:1024], psB[:, 0:512], psB[:, 512:1024]]
                    for i in range(4):
                        win_mm(wb, i, tl, outs[i])
                    for ch in range(2):
                        cp("act", QB[:, ch, :W], outs[ch][:, :W])
                        act(SGX[:, ch, :W], outs[2 + ch][:, :W], AF.Silu)
                    if tl["kind"] == "P":
                        scb = [(psC, psD), (psB[:, 0:512], psB[:, 512:1024])]

                        def xa_S(h):
                            hc, e = h // 2, h % 2
                            pe_ = slice(64 * e, 64 * e + 64)
                            for mt in range(2):
                                mm(scb[h % 2][mt][:, :W], KTm[pe_, hc, mt * 128:(mt + 1) * 128], QB[pe_, hc, :W], tp=(64 * e, 0))

                        def xa_E(h):
                            for mt in range(2):
                                act(ET2[h % 2][:, mt, :W], scb[h % 2][mt][:, :W], AF.Exp, scale=0.125)

                        def xa_PVD(h):
                            hc, e = h // 2, h % 2
                            pe_ = slice(64 * e, 64 * e + 64)
                            for mt in range(2):
                                mm(psE[pe_, :W], VTm[:, mt, h * 64:(h + 1) * 64], ET2[h % 2][:, mt, :W], start=(mt == 0), stop=(mt == 1), tp=(0, 64 * e))
                            for mt in range(2):
                                mm(psA[pe_, :W], ONES64, ET2[h % 2][:, mt, :W], start=(mt == 0), stop=(mt == 1), tp=(0, 64 * e))

                        xa_S(0); xa_S(1)
                        for h in range(4):
                            xa_E(h)
                            xa_PVD(h)
                            if h + 2 < 4:
                                xa_S(h + 2)
                            if h % 2 == 1:
                                hc = h // 2
                                recip(RD[:, :W], psA[:, :W])
                                tt("dve", TO[:, :W], psE[:, :W], RD[:, :W], ALU.mult)
                                tt("pool", MIX[:, 6 + hc, lc:lc + W], TO[:, :W], SGX[:, hc, :W], ALU.mult)
                    else:
                        QM = cb.get(2 * 256).rearrange("p (c b t) -> p c b t", c=2, b=16)
                        PTM = cb.get(8 * 256).rearrange("p (x b t) -> p x b t", x=8, b=16)
                        PS_ = cf.get(1024).rearrange("p (h m) -> p h m", h=4)
                        MXs = cf.get(4); NBs = cf.get(4); SMs = cf.get(4); OS = cf.get(256)
                        E16 = EYE16.rearrange("p (b t) -> p b t", b=16)
                        for hc in range(2):
                            tt("dve", QM[:, hc], QB[:, hc, 0:16].unsqueeze(1).to_broadcast([128, 16, 16]), E16, ALU.mult)
                        for bg in range(4):
                            KC = KC2[bg % 2]
                            if bg + 1 < 4:
                                loadK(bg + 1)
                            for h in (0, 2, 1, 3):
                                hc, e = h // 2, h % 2
                                pe_ = slice(64 * e, 64 * e + 64)
                                for bb in range(4):
                                    b = bg * 4 + bb
                                    mm(psA[0:16, h * 256:(h + 1) * 256], QM[pe_, hc, b, :], KC[pe_, bb, hc, :], start=(b == 0 and h % 2 == 0), stop=(b == 15), tp=(64 * e, 0), sgc=True)
                        SCV = psA[0:16, :].rearrange("p (h m) -> p h m", h=4)
                        rmax(MXs[0:16, :], SCV)
                        ts("dve", NBs[0:16, :], MXs[0:16, :], -0.125, ALU.mult)
                        for h in range(4):
                            act(PS_[0:16, h, :], SCV[:, h, :], AF.Exp, bias=NBs[0:16, h:h + 1], scale=0.125)
                        rsum(SMs[0:16, :], PS_[0:16])
                        for h in range(4):
                            for mt in range(2):
                                tr(psC[:, (h * 2 + mt) * 16:(h * 2 + mt) * 16 + 16], PS_[0:16, h, mt * 128:(mt + 1) * 128], ID32[0:16, 0:16])
                        tt("dve", PTM[:], psC[:, 0:128].rearrange("p (x t) -> p x t", x=8).unsqueeze(2).to_broadcast([128, 8, 16, 16]),
                           E16.unsqueeze(1).to_broadcast([128, 8, 16, 16]), ALU.mult)
                        for bg in range(4):
                            VC = VC2[bg % 2]
                            if bg + 1 < 4:
                                loadV(bg + 1)
                            for bb in range(4):
                                b = bg * 4 + bb
                                for h in range(4):
                                    for mt in range(2):
                                        mm(psD[0:16, h * 64:(h + 1) * 64], PTM[:, h * 2 + mt, b, :], VC[:, bb, mt, h * 64:(h + 1) * 64],
                                           start=(b == 0 and mt == 0 and h == 0), stop=(b == 15 and mt == 1), sgc=True)
                        recip(SMs[0:16, :], SMs[0:16, :])
                        tt("dve", OS[0:16, :].rearrange("p (h d) -> p h d", h=4), psD[0:16, 0:256].rearrange("p (h d) -> p h d", h=4),
                           SMs[0:16, :].unsqueeze(2).to_broadcast([16, 4, 64]), ALU.mult)
                        for hc in range(2):
                            tr(psE[:, hc * 16:(hc + 1) * 16], OS[0:16, hc * 128:(hc + 1) * 128], ID32[0:16, 0:16])
                            tt("dve", MIX[:, 6 + hc, lc:lc + W], psE[:, hc * 16:(hc + 1) * 16], SGX[:, hc, :W], ALU.mult)
                mark("xa_done")
                cf.reset(); cb.reset()
                wb = get_w(l, si, 2)
                XSL = cf.get(512); A1 = cf.get(512)
                SSH = cf.get(13 * 16).rearrange("p (j b) -> p j b", j=13)
                SHS = cf.get(13 * 16).rearrange("p (j b) -> p j b", j=13)
                SV6 = cf.get(4 * 6 * 16).rearrange("p (c v b) -> p c v b", c=4, v=6)
                SGGS = cf.get(4 * 16).rearrange("p (c b) -> p c b", c=4)
                if has_s:
                    dma("sp", SSH, sshT[l].rearrange("(j p) b -> p j b", p=128))
                for tl in tls:
                    W, lc = tl["W"], tl["lc"]
                    win_mm(wb, 0, tl, psC)
                    tshift(l, 12, psC, tl, XSL, A1, SSH, SHS)
                    act(LTS[0:64, lc:lc + W], XSL[0:64, :W], AF.Tanh)
                    cp("dve", LTS[64:128, lc:lc + W], XSL[64:128, :W])
                mark("lora_done")
                f_mark, b_mark = cf.o, cb.o
                XR = XSL
                XK = cf.get(512); XV = cf.get(512)
                SG = cf.get(512); AA = cf.get(512); RN = cf.get(512); KKN = cf.get(512); KP = cf.get(512); KA = cf.get(512)
                ENW = cf.get(512); EWt = cf.get(512)
                CUM = RN
                ECW = RN
                YB2 = [cf.get(512), cf.get(512)]
                HS2 = [cf.get(512).rearrange("p (n i) -> p n i", n=8), cf.get(512).rearrange("p (n i) -> p n i", n=8)]
                WC2 = [cf.get(8), cf.get(8)]
                TMPS = cf.get(64)
                T1 = SG
                T2 = AA
                YS = KKN
                KK2 = cb.get(512); RKb = KK2
                BT = cb.get(512); KT_ = cb.get(512); BH = cb.get(512); KH = cb.get(512); VB = cb.get(512)
                MPa_f = cb.get(1024)
                MPa = MPa_f.rearrange("p (x t) -> p x t", x=8)
                MPb = cb.get(1024).rearrange("p (x t) -> p x t", x=8)
                La = cb.get(512).rearrange("p (x t) -> p x t", x=8)
                Lb = cb.get(512).rearrange("p (x t) -> p x t", x=8)
                YBF = MPa_f[:, 0:512]; YSQ = MPa_f[:, 512:1024]
                AR = cb.get(1024).rearrange("p (n x) -> p n x", n=8)
                ATc = cb.get(512)
                VT = cb.get(512).rearrange("p (r f) -> p r f", r=4)
                KHT = cb.get(512).rearrange("p (r f) -> p r f", r=4)
                BHT = cb.get(512).rearrange("p (r f) -> p r f", r=4)
                ATT = cb.get(512).rearrange("p (r f) -> p r f", r=4)
                ATAK = cb.get(512).rearrange("p (x t) -> p x t", x=8)
                ATRB = cb.get(512).rearrange("p (x t) -> p x t", x=8)
                ATRK = cb.get(512).rearrange("p (x t) -> p x t", x=8)
                PTF = cb.get(512).rearrange("p (x t) -> p x t", x=8)
                X0b = cb.get(512).rearrange("p (x t) -> p x t", x=8)
                VPb = cb.get(512).rearrange("p (x t) -> p x t", x=8)
                APb = cb.get(512).rearrange("p (x t) -> p x t", x=8)
                G0T2 = [cb.get(512).rearrange("p (n j) -> p n j", n=8) for _p in range(2)]
                RH2 = [cb.get(512).rearrange("p (n t) -> p n t", n=8) for _p in range(2)]
                Y02 = [cb.get(512), cb.get(512)]
                SGG2 = [cb.get(512), cb.get(512)]
                units = [(c, tl) for c in range(4) for tl in tls]
                wbs = {}
                ppar = {}
                for _ui, (_c, _tl) in enumerate(units):
                    ppar[_ui] = sum(1 for (_c2, _t2) in units[:_ui] if _t2["kind"] == "P") % 2

                def front(ui):
                    c, tl = units[ui]
                    par = ppar[ui]
                    SGG, YB, HS, WC = SGG2[par], YB2[par], HS2[par], WC2[par]
                    G0T, RH, Y0 = G0T2[par], RH2[par], Y02[par]
                    W, gx, lc = tl["W"], tl["gx"], tl["lc"]
                    isP = tl["kind"] == "P"
                    if c not in wbs:
                        wbs[c] = get_w(l, si, 3 + c)
                    wb = wbs[c]
                    outs = [psA[:, 0:512], psA[:, 512:1024], psB[:, 0:512], psB[:, 512:1024]]
                    mark("u_start")
                    for i in range(4):
                        win_mm(wb, i, tl, outs[i])
                    yield
                    tshift(l, 4 + c, outs[1], tl, XK, A1, SSH, SHS)
                    act(KK2[:, :W], XK[:, :W], AF.Square, scale=V(l, O_KK + c))
                    tshift(l, 8 + c, outs[2], tl, XV, A1, SSH, SHS)
                    act(SGG[:, :W] if isP else SGGS[:, c, :], outs[3][:, :W], AF.Silu)
                    tshift(l, c, outs[0], tl, XR, A1, SSH, SHS)
                    yield
                    mm(psB[:, 0:W], LORA2[0:64, c * 128:(c + 1) * 128], LTS[0:64, lc:lc + W], tp=(0, 0))
                    mm(psB[:, 512:512 + W], LORA2[64:128, c * 128:(c + 1) * 128], LTS[64:128, lc:lc + W], tp=(64, 0))
                    mm(psC[:, :W], BONES, KK2[:, :W])
                    act(SG[:, :W], psB[:, 0:W], AF.Sigmoid, bias=V(l, O_W0 + c))
                    ts("dve", RN[:, :W], psC[:, :W], 1e-24, ALU.max)
                    act(AA[:, :W], psB[:, 512:512 + W], AF.Sigmoid, bias=V(l, O_A0 + c))
                    act(RN[:, :W], RN[:, :W], AF.Sqrt)
                    recip(RN[:, :W], RN[:, :W])
                    act(KP[:, :W], AA[:, :W], AF.Identity, bias=V(l, O_OMKA + c), scale=V(l, O_KA + c))
                    stt("dve", KKN[:, :W], XK[:, :W], V(l, O_KK + c), RN[:, :W], ALU.mult, ALU.mult)
                    tt("pool", KP[:, :W], KP[:, :W], XK[:, :W], ALU.mult)
                    tt("pool", KA[:, :W], KKN[:, :W], AA[:, :W], ALU.mult)
                    if not isP:
                        cp("pool", SV6[:, c, 0, :], XR[:, :W])
                        act(SV6[:, c, 1, :], SG[:, :W], AF.Exp, scale=-LWC)
                        cp("pool", SV6[:, c, 2, :], KP[:, :W])
                        cp("pool", SV6[:, c, 3, :], XV[:, :W])
                        cp("pool", SV6[:, c, 4, :], KKN[:, :W])
                        cp("pool", SV6[:, c, 5, :], KA[:, :W])
                        yield
                        return
                    P.add("dve", lambda e, o=CUM[:, :W], m=SEGM[:, :W], d=SG[:, :W]: e.tensor_tensor_scan(out=o, data0=m, data1=d, initial=0.0, op0=ALU.mult, op1=ALU.add),
                          R=[SEGM[:, :W], SG[:, :W]], W=[CUM[:, :W]])
                    stt("dve", RKb[:, :W], XR[:, :W], V(l, O_RK + c), KP[:, :W], ALU.mult, ALU.mult)
                    act(EWt[:, :W], CUM[:, :W], AF.Exp, scale=-LWC)
                    act(ENW[:, :W], CUM[:, :W], AF.Exp, scale=LWC)
                    yield
                    mm(psC[:, :W], BONES, RKb[:, :W])
                    v3 = lambda a_: a_[:, :W].rearrange("p (n s) -> p n s", s=64)
                    tt("dve", v3(ECW), v3(ENW), v3(EWt)[:, :, 63:64].to_broadcast([128, 8, 64]), ALU.mult)
                    cp("pool", WC[:, 0:8], v3(EWt)[:, :, 63])
                    tt("pool", BT[:, :W], KA[:, :W], ENW[:, :W], ALU.mult)
                    tt("pool", KT_[:, :W], KP[:, :W], ENW[:, :W], ALU.mult)
                    tt("pool", AR[:, :, 64:128], v3(XR), v3(EWt), ALU.mult)
                    stt("dve", AR[:, :, 1:64], v3(KKN)[:, :, 1:64], -1.0, v3(EWt)[:, :, 0:63], ALU.mult, ALU.mult)
                    ts("pool", AR[:, :, 0:1], v3(KKN)[:, :, 0:1], -1.0, ALU.mult)
                    cp("pool", VB[:, :W], XV[:, :W])
                    tt("dve", YB[:, :W], psC[:, :W], XV[:, :W], ALU.mult)
                    tt("pool", KH[:, :W], KP[:, :W], ECW[:, :W], ALU.mult)
                    tt("pool", BH[:, :W], KA[:, :W], ECW[:, :W], ALU.mult)
                    cp("pool", v3(ATc), AR[:, :, 0:64])
                    yield
                    for e in range(2):
                        pe_ = slice(64 * e, 64 * e + 64)
                        for n in range(8):
                            q, r = n % 2, n // 2
                            pq = slice(64 * q, 64 * q + 64)
                            x = r * 2 + e
                            tok = slice(n * 64, (n + 1) * 64)
                            mm(psA[pq, x * 128:(x + 1) * 128], BT[pe_, tok], AR[pe_, n, :], tp=(64 * e, 64 * q))
                            mm(psB[pq, x * 128:(x + 1) * 128], KT_[pe_, tok], AR[pe_, n, :], tp=(64 * e, 64 * q))
                            mm(psC[pq, x * 64:(x + 1) * 64], AR[pe_, n, 0:64], BT[pe_, tok], tp=(64 * e, 64 * q))
                    for src, half in ((VB, 0), (KH, 1)):
                        for r in range(4):
                            tr(psT[:, half * 512 + r * 128: half * 512 + (r + 1) * 128], src[:, r * 128:(r + 1) * 128], IDB)
                    A8 = psA[:, :].rearrange("p (x t) -> p x t", x=8)
                    B8 = psB[:, :].rearrange("p (x t) -> p x t", x=8)
                    C8 = psC[:, :].rearrange("p (x t) -> p x t", x=8)
                    bc = lambda m: m.unsqueeze(1).to_broadcast([128, 8, 64])
                    tt("dve", MPa[:, :, 0:64], A8[:, :, 0:64], bc(MSU), ALU.mult)
                    tt("dve", La[:], C8, bc(MSL), ALU.mult)
                    tt("pool", MPa[:, :, 64:128], MPa[:, :, 0:64], bc(ID64), ALU.add)
                    cp("act", VT[:], psT[:, 0:512].rearrange("p (r f) -> p r f", r=4))
                    cp("act", KHT[:], psT[:, 512:1024].rearrange("p (r f) -> p r f", r=4))
                    tt("dve", ATRB[:], A8[:, :, 64:128], bc(MUU), ALU.mult)
                    tt("dve", ATAK[:], B8[:, :, 0:64], bc(MSU), ALU.mult)
                    tt("dve", ATRK[:], B8[:, :, 64:128], bc(MUU), ALU.mult)
                    yield
                    MPc, MPn, Lc, Ln = MPa, MPb, La, Lb
                    for k in range(6):
                        PSM = psA if k % 2 == 0 else psB
                        LO = psB if k % 2 == 0 else psA
                        M8 = PSM[:, :].rearrange("p (x t) -> p x t", x=8)
                        L8 = LO[:, :].rearrange("p (x t) -> p x t", x=8)
                        for xh in range(2):
                            xs = slice(xh * 4, xh * 4 + 4)
                            for q in ((0, 1) if (k + xh) % 2 == 0 else (1, 0)):
                                pq = slice(64 * q, 64 * q + 64)
                                tpq = (64 * q, 64 * q)
                                for x in range(xh * 4, xh * 4 + 4):
                                    if k == 0:
                                        mm(PSM[pq, x * 128:x * 128 + 64], Lc[pq, x, :], MPc[pq, x, 0:64], tp=tpq)
                                    elif k < 5:
                                        mm(PSM[pq, x * 128:(x + 1) * 128], Lc[pq, x, :], MPc[pq, x, :], tp=tpq)
                                    else:
                                        mm(PSM[pq, x * 128:x * 128 + 64], Lc[pq, x, :], MPc[pq, x, 64:128], tp=tpq)
                                if k < 5:
                                    for x in range(xh * 4, xh * 4 + 4):
                                        mm(LO[pq, x * 128:x * 128 + 64], MPc[pq, x, 0:64], Lc[pq, x, :], tp=tpq)
                            if k == 0:
                                cp("act", MPn[:, xs, 0:64], M8[:, xs, 0:64])
                                cp("pool", MPn[:, xs, 64:128], MPc[:, xs, 64:128])
                                cp("act", Ln[:, xs, :], L8[:, xs, 0:64])
                            elif k < 5:
                                cp("act", MPn[:, xs, 0:64], M8[:, xs, 0:64])
                                tt("dve", MPn[:, xs, 64:128], M8[:, xs, 64:128], MPc[:, xs, 64:128], ALU.add)
                                cp("act", Ln[:, xs, :], L8[:, xs, 0:64])
                            else:
                                tt("dve", PTF[:, xs, :], M8[:, xs, 0:64], MPc[:, xs, 64:128], ALU.add)
                        if k == 0:
                            for r in range(4):
                                tr(psT[:, r * 128:(r + 1) * 128], BH[:, r * 128:(r + 1) * 128], IDB)
                                tr(psT[:, 512 + r * 128:512 + (r + 1) * 128], ATc[:, r * 128:(r + 1) * 128], IDB)
                            cp("act", BHT[:], psT[:, 0:512].rearrange("p (r f) -> p r f", r=4))
                            cp("act", ATT[:], psT[:, 512:1024].rearrange("p (r f) -> p r f", r=4))
                        if k == 1:
                            for q in range(2):
                                pq = slice(64 * q, 64 * q + 64)
                                for r in range(4):
                                    for e in range(2):
                                        x = r * 2 + e
                                        mm(psC[pq, x * 64:(x + 1) * 64], ATAK[pq, x, :], VT[pq, r, e * 64:(e + 1) * 64], tp=(64 * q, 64 * q))
                            cp("act", X0b[:], psC[:, :].rearrange("p (x t) -> p x t", x=8))
                        MPc, MPn, Lc, Ln = MPn, MPc, Ln, Lc
                        yield
                    for q in range(2):
                        pq = slice(64 * q, 64 * q + 64)
                        for r in range(4):
                            for e in range(2):
                                x = r * 2 + e
                                mm(psA[pq, x * 64:(x + 1) * 64], PTF[pq, x, :], X0b[pq, x, :], tp=(64 * q, 64 * q))
                                mm(psB[pq, x * 64:(x + 1) * 64], PTF[pq, x, :], ATT[pq, r, e * 64:(e + 1) * 64], tp=(64 * q, 64 * q))
                    cp("act", VPb[:], psA[:, 0:512].rearrange("p (x t) -> p x t", x=8))
                    cp("act", APb[:], psB[:, 0:512].rearrange("p (x t) -> p x t", x=8))
                    yield
                    for q in range(2):
                        pq = slice(64 * q, 64 * q + 64)
                        for r in range(4):
                            n = 2 * r + q
                            for e in range(2):
                                x = r * 2 + e
                                pe_ = slice(64 * e, 64 * e + 64)
                                fe = slice(e * 64, (e + 1) * 64)
                                ns = slice(n * 64, (n + 1) * 64)
                                tpo = (64 * q, 64 * e)
                                mm(psC[pe_, ns], APb[pq, x, :], BHT[pq, r, fe], tp=tpo)
                                mm(psA[pe_, 512 + n * 64:512 + (n + 1) * 64], BHT[pq, r, fe], VPb[pq, x, :], start=True, stop=False, tp=tpo)
                                mm(psA[pe_, 512 + n * 64:512 + (n + 1) * 64], KHT[pq, r, fe], VT[pq, r, fe], start=False, stop=True, tp=tpo)
                                mm(psB[pe_, 512 + n * 64:512 + (n + 1) * 64], APb[pq, x, :], ATRB[pq, x, :], tp=tpo)
                                mm(psA[pe_, ns], VPb[pq, x, :], ATRB[pq, x, :], start=True, stop=False, tp=tpo)
                                mm(psA[pe_, ns], VT[pq, r, fe], ATRK[pq, x, :], start=False, stop=True, tp=tpo)
                    cp("act", G0T[:], psC[:, :].rearrange("p (n j) -> p n j", n=8))
                    cp("dve", HS[:], psA[:, 512:1024].rearrange("p (n i) -> p n i", n=8))
                    tt("dve", RH[:], psB[:, 512:1024].rearrange("p (n t) -> p n t", n=8), AR[:, :, 64:128], ALU.add)
                    cp("act", Y0[:, :W], psA[:, 0:512])
                    yield

                def adv(g, k):
                    if g is None:
                        return
                    for _ in range(k):
                        try:
                            next(g)
                        except StopIteration:
                            return

                def back(ui, nxt):
                    c, tl = units[ui]
                    par = ppar[ui]
                    SGG, YB, HS, WC = SGG2[par], YB2[par], HS2[par], WC2[par]
                    G0T, RH, Y0 = G0T2[par], RH2[par], Y02[par]
                    W, gx, lc = tl["W"], tl["gx"], tl["lc"]
                    mark("u_dbl")
                    for n in range(8):
                        tok = slice(n * 64, (n + 1) * 64)
                        ns = slice(n * 64, (n + 1) * 64)
                        stt("dve", TMPS, ST[:, c, :], WC[:, n:n + 1], HS[:, n, :], ALU.mult, ALU.add)
                        for e in range(2):
                            pe_ = slice(64 * e, 64 * e + 64)
                            mm(psD[pe_, ns], G0T[pe_, n, :], STb[pe_, c, :], tp=(64 * e, 64 * e))
                            mm(psE[pe_, tok], STb[pe_, c, :], RH[pe_, n, :], tp=(64 * e, 64 * e))
                        tt("dve", ST[:, c, :], TMPS, psD[:, ns], ALU.add)
                        cp("act", STb[:, c, :], ST[:, c, :])
                        adv(nxt, 4 if n < 3 else 1)
                    adv(nxt, 1000)
                    if tl["i"] == 3:
                        dma("sp", wkvp[l, c], ST[:, c, :])
                    mark("u_state")
                    tt("dve", YS[:, :W], psE[:, :W], Y0[:, :W], ALU.add)
                    cp("act", YBF[:, :W], YS[:, :W])
                    act(YSQ[:, :W], YS[:, :W], AF.Square)
                    mm(psD[:, 0:W], BMEAN, YBF[:, :W])
                    cp("act", T1[:, :W], psD[:, 0:W])
                    mm(psD[:, 0:W], BMEAN, YSQ[:, :W])
                    tt("pool", T2[:, :W], T1[:, :W], T1[:, :W], ALU.mult)
                    tt("dve", T2[:, :W], psD[:, 0:W], T2[:, :W], ALU.subtract)
                    act(T2[:, :W], T2[:, :W], AF.Sqrt, bias=EPSC[:, 1:2])
                    recip(T2[:, :W], T2[:, :W])
                    tt("pool", T1[:, :W], YS[:, :W], T1[:, :W], ALU.subtract)
                    tt("pool", T1[:, :W], T1[:, :W], T2[:, :W], ALU.mult)
                    act(T1[:, :W], T1[:, :W], AF.Identity, bias=V(l, O_LNB + c), scale=V(l, O_LNG + c))
                    tt("pool", T1[:, :W], T1[:, :W], YB[:, :W], ALU.add)
                    tt("pool", MIX[:, 2 + c, lc:lc + W], T1[:, :W], SGG[:, :W], ALU.mult)

                def chain(ui):
                    j = ui
                    while j < len(units):
                        yield from front(j)
                        if units[j][1]["kind"] == "P":
                            break
                        j += 1

                g0 = chain(0)
                adv(g0, 1000)
                for ui in range(len(units)):
                    if units[ui][1]["kind"] == "P":
                        back(ui, chain(ui + 1) if ui + 1 < len(units) else None)
                mark("rw_done")
                if any(t["kind"] == "P" and t["i"] == 3 for t in tls):
                    dma("sp", shiftp[l], PREV[:])
                if has_s:
                    stl = [t for t in tls if t["kind"] == "S"][0]
                    lc = stl["lc"]
                    dma("sp", shifts[l], SHS)
                    cf.o, cb.o = f_mark, b_mark
                    TM6 = cf.get(6 * 128).rearrange("p (v f) -> p v f", v=6)
                    for c in range(4):
                        for v in range(6):
                            tr(psA[0:16, v * 128:(v + 1) * 128], SV6[:, c, v, :], ID32)
                        cp("act", TM6[0:16], psA[0:16, 0:768].rearrange("p (v f) -> p v f", v=6))
                        for e in range(2):
                            dma("sp", scr6[:, 2 * c + e], TM6[0:16, :, e * 64:(e + 1) * 64])
                    V6 = cf.get(384).rearrange("p (v j) -> p v j", v=6)
                    BH3 = cf.get(192).rearrange("p (v j) -> p v j", v=3)
                    SS = cf.get(2048).rearrange("p (i j) -> p i j", i=32)
                    TP = cf.get(2048).rearrange("p (i j) -> p i j", i=32)
                    UU = cf.get(64); YY = cf.get(64); YC = cf.get(64); SC1 = cf.get(8); TMY = cf.get(512)
                    dma("sp", V6, scr6.rearrange("b h v j -> (b h) v j"))
                    dma("sp", BH3, bhv[l].rearrange("p (v j) -> p v j", v=3))
                    bj = lambda a: a.unsqueeze(1).to_broadcast([128, 32, 64])
                    bi = lambda a: a.unsqueeze(2).to_broadcast([128, 32, 64])
                    for ih in range(2):
                        hs = slice(ih * 32, ih * 32 + 32)
                        dma("sp", SS, swkv[l][:, ih * 2048:(ih + 1) * 2048].rearrange("p (i j) -> p i j", i=32))
                        tt("dve", TP, SS, bj(V6[:, 4, :]), ALU.mult)
                        rsum(UU[:, hs], TP)
                        tt("pool", SS, SS, bj(V6[:, 1, :]), ALU.mult)
                        tt("dve", TP, bi(UU[:, hs]), bj(V6[:, 5, :]), ALU.mult)
                        tt("pool", SS, SS, TP, ALU.subtract)
                        tt("dve", TP, bi(V6[:, 3, hs]), bj(V6[:, 2, :]), ALU.mult)
                        tt("pool", SS, SS, TP, ALU.add)
                        dma("sp", wkvs[l][:, ih * 2048:(ih + 1) * 2048].rearrange("p (i j) -> p i j", i=32), SS)
                        tt("dve", TP, SS, bj(V6[:, 0, :]), ALU.mult)
                        rsum(YY[:, hs], TP)
                    rsum(SC1[:, 0:1], YY)
                    ts("dve", SC1[:, 0:1], SC1[:, 0:1], 1.0 / 64, ALU.mult)
                    ts("dve", YC, YY, SC1[:, 0:1], ALU.subtract)
                    tt("dve", YY, YC, YC, ALU.mult)
                    rsum(SC1[:, 1:2], YY)
                    ts("dve", SC1[:, 1:2], SC1[:, 1:2], 1.0 / 64, ALU.mult)
                    act(SC1[:, 1:2], SC1[:, 1:2], AF.Sqrt, bias=EPSC[:, 1:2])
                    recip(SC1[:, 1:2], SC1[:, 1:2])
                    ts("dve", YC, YC, SC1[:, 1:2], ALU.mult)
                    tt("dve", YC, YC, BH3[:, 0, :], ALU.mult)
                    tt("dve", YC, YC, BH3[:, 1, :], ALU.add)
                    tt("dve", YY, V6[:, 0, :], V6[:, 2, :], ALU.mult)
                    tt("dve", YY, YY, BH3[:, 2, :], ALU.mult)
                    rsum(SC1[:, 2:3], YY)
                    stt("dve", YC, V6[:, 3, :], SC1[:, 2:3], YC, ALU.mult, ALU.add)
                    dma("sp", scry.rearrange("b (h i) -> (b h) i", h=8), YC)
                    dma("sp", TMY[0:16, :], scry)
                    for c in range(4):
                        tr(psC[:, c * 16:(c + 1) * 16], TMY[0:16, c * 128:(c + 1) * 128], ID32[0:16, 0:16])
                        tt("dve", MIX[:, 2 + c, lc:lc + NS], psC[:, c * 16:(c + 1) * 16], SGGS[:, c, :], ALU.mult)
                mark("rws_done")
                banks = [psA[:, 0:512], psA[:, 512:1024], psB[:, 0:512], psB[:, 512:1024], psD, psE]
                bk = 0
                nxt_ls = (l, si + 1) if si + 1 < len(SEGS) else ((l + 1, 0) if l + 1 < NL else None)
                ng = norm_gen(nxt_ls[0], tiles_of(SEGS[nxt_ls[1]])) if nxt_ls is not None else None
                for dc in range(8):
                    wo = WOB[dc % 2]
                    for tl in tls:
                        W, gx, lc = tl["W"], tl["gx"], tl["lc"]
                        pso = banks[bk % 6]
                        bk += 1
                        for mc in range(8):
                            mm(pso[:, :W], wo[:, mc, :], MIX[:, mc, lc:lc + W], start=(mc == 0), stop=(mc == 7))
                        tt("dve", X[:, dc, gx:gx + W], X[:, dc, gx:gx + W], pso[:, :W], ALU.add)
                        if ng is not None:
                            next(ng, None)
                    if dc + 2 < 8:
                        dma("pool", wo[:], wout[l, dc + 2])
                if ng is not None:
                    for _ in ng:
                        pass

        mark("out_done")
        cf.reset(); cb.reset()
        SQT = [cb.get(512), cb.get(512)]
        RSF = cf.get(512)
        YOB = [cf.get(512) for _ in range(4)]
        yi = 0
        for tl in tiles_of([("P", 0), ("P", 1), ("P", 2), ("P", 3), ("S", 0)]):
            W, gx = tl["W"], tl["gx"]
            rms_rstd(lambda kc: X[:, kc, gx:gx + W], W, RSF[:, :W], SQT, psC)
            for kc in range(8):
                yo = YOB[yi % 4]
                yi += 1
                stt(EW(), yo[:, :W], X[:, kc, gx:gx + W], FG[:, kc:kc + 1], RSF[:, :W], ALU.mult, ALU.mult)
                if tl["kind"] == "P":
                    dma("sp", yT[kc * 128:(kc + 1) * 128, gx:gx + W], yo[:, :W])
                else:
                    dma("sp", ysT[kc * 128:(kc + 1) * 128, :], yo[:, :W])
        import os as _os
        _stop = int(_os.environ.get("K_STOP", "0"))
        if _stop:
            del P.ops[_stop:]
            for k_ in P.lanes:
                P.lanes[k_] = [o for o in P.lanes[k_] if o.id < _stop]
            P.lanes = {k_: v_ for k_, v_ in P.lanes.items() if v_}
        P.add("sp", lambda e: None, R=[])
        fin = P.ops[-1]
        for lane, lst in P.lanes.items():
            fin.deps.add(lst[-1])

        sems = {}
        for nm in ["pe", "act", "dve", "pool"] + ["sp%d" % i for i in range(8)] + ["pool%d" % i for i in range(8)] + ["act%d" % i for i in range(4)]:
            sems[nm] = es.enter_context(nc.semaphore("s_" + nm))
        block = es.enter_context(nc.Block())
        P.emit(nc, block, sems)
    return nc, P


def _consts():
    cst = np.zeros((128, NCST), np.float32)
    p = np.arange(128)
    cst[:, 0:128] = np.eye(128, dtype=np.float32)
    t = np.arange(64)
    pm = (p % 64)[:, None]
    cst[:, 128:192] = (pm < t[None, :])
    cst[:, 192:256] = (pm <= t[None, :])
    cst[:, 256:320] = (t[None, :] < pm)
    cst[:, 320:576] = np.eye(16, dtype=np.float32).reshape(1, 256)
    wins = np.array([[2, 8], [4, 16]], np.float32)
    for ch in range(2):
        for half in range(2):
            w = wins[half, ch]
            cst[half * 64:(half + 1) * 64, 576 + ch] = 1.0 / w
            cst[half * 64:(half + 1) * 64, 578 + ch * 16: 578 + ch * 16 + 16] = 1.0 / np.minimum(np.arange(16) + 1, w)
    seg = np.ones(512, np.float32)
    seg[::64] = 0.0
    cst[:, 610:1122] = seg[None, :]
    cb = np.zeros((128, NCSTB), np.float32)
    cb[:, 0:128] = np.eye(128)
    cb[:, 128:256] = 1.0 / 1024
    blk = np.zeros((128, 128), np.float32)
    blk[0:64, 0:64] = 1
    blk[64:128, 64:128] = 1
    cb[:, 256:384] = blk
    cb[:, 384:512] = blk / 64
    cb[:, 512:576] = (pm == t[None, :])
    cb[:, 576:640] = 1.0
    return cst, cb


def _fm(v, n):
    return np.ascontiguousarray(v.reshape(n, 128).T)


def kernel(x_prompt, x_sample, mem_prompt, state_pool, state_shift, state_wkv, cache_mem_k, cache_mem_v,
           norm_g, w_in, w_out, pool_w, pool_scale, shift_mu, w0, w_w2, a0, w_a2, k_k, k_a, r_k,
           ln_x_g, ln_x_b, mem_norm_g, w_kv, final_norm_g, _NL=L_ALL, _trace=False):
    f = lambda a: np.asarray(a, np.float32)
    x_prompt, x_sample, mem_prompt = f(x_prompt), f(x_sample), f(mem_prompt)
    L = L_ALL
    cst, cstb = _consts()
    vec = np.zeros((128, VL * L + 8), np.float32)
    for l in range(L):
        b = l * VL
        vec[:, b + 0:b + 8] = _fm(f(norm_g)[l], 8)
        vec[:, b + 8:b + 21] = _fm(f(shift_mu)[l], 13)
        vec[:, b + 34:b + 38] = _fm(f(w0)[l], 4)
        vec[:, b + 38:b + 42] = _fm(f(a0)[l], 4)
        vec[:, b + 42:b + 46] = _fm(f(k_k)[l], 4)
        vec[:, b + 46:b + 50] = _fm(f(k_a)[l], 4)
        vec[:, b + 54:b + 58] = _fm(f(r_k)[l].reshape(-1), 4)
        vec[:, b + 58:b + 62] = _fm(f(ln_x_g)[l], 4)
        vec[:, b + 62:b + 66] = _fm(f(ln_x_b)[l], 4)
        vec[:, b + 66:b + 68] = _fm(f(pool_scale)[l], 2)
        vec[:, b + 68:b + 76] = _fm(f(mem_norm_g)[l], 8)
    vec[:, VL * L:VL * L + 8] = _fm(f(final_norm_g), 8)
    w_in = f(w_in)
    win = np.ascontiguousarray(w_in.reshape(L, 8, 128, 25, 128)[:, :, :, FC_ORDER, :].transpose(0, 3, 2, 1, 4))
    wout = np.ascontiguousarray(f(w_out).reshape(L, 8, 128, 8, 128).transpose(0, 3, 2, 1, 4))
    wkv = np.ascontiguousarray(f(w_kv).reshape(L, 8, 128, 512).transpose(0, 2, 1, 3))
    lora2 = np.ascontiguousarray(np.concatenate([f(w_w2), f(w_a2)], axis=1))
    pw = f(pool_w)
    pwb = np.zeros((L, 2, 128, 128), np.float32)
    for ch in range(2):
        pwb[:, ch, 0:64, 0:64] = pw[:, 2 * ch]
        pwb[:, ch, 64:128, 64:128] = pw[:, 2 * ch + 1]
    bh = np.stack([f(ln_x_g).reshape(L, 8, 64), f(ln_x_b).reshape(L, 8, 64), f(r_k).reshape(L, 8, 64)], axis=2)
    bhv = np.ascontiguousarray(np.broadcast_to(bh[:, None], (L, 16, 8, 3, 64)).reshape(L, 128, 192))
    shared = dict(win=win, wout=wout, wkv=wkv, vec=vec, lora2=lora2, pwb=pwb, cst=cst, cstb=cstb, bhv=bhv)
    sp, ss, sw, ck, cvv = f(state_pool), f(state_shift), f(state_wkv), f(cache_mem_k), f(cache_mem_v)
    in_maps = []
    for c in range(8):
        bs = slice(16 * c, 16 * c + 16)
        m = dict(shared)
        m["xT"] = np.ascontiguousarray(x_prompt[c].T)
        m["xsT"] = np.ascontiguousarray(x_sample[bs, 0, :].T)
        m["memT"] = np.ascontiguousarray(mem_prompt[c].T)
        m["spT"] = np.ascontiguousarray(sp[:, bs].transpose(0, 3, 1, 2))
        m["sshT"] = np.ascontiguousarray(ss[:, bs].transpose(0, 2, 1))
        m["swkv"] = np.ascontiguousarray(sw[:, bs].reshape(L, 128, 4096))
        m["ckT"] = np.ascontiguousarray(ck[:, bs].reshape(L, 16, 256, 2, 128).transpose(0, 1, 3, 4, 2))
        m["cv"] = np.ascontiguousarray(cvv[:, bs].reshape(L, 16, 256, 256))
        in_maps.append(m)
    nc, _ = build_program(_NL)
    res = run_bass_kernel_spmd(nc, in_maps, core_ids=list(range(8)), **({"trace": True} if _trace else {}))
    R = res.results
    y_prompt = np.stack([R[c]["yT"].T for c in range(8)])
    y_sample = np.concatenate([R[c]["ysT"].T for c in range(8)])[:, None, :]
    pool_prompt = np.stack([R[c]["poolp"].transpose(0, 2, 1) for c in range(8)], axis=1)
    shift_prompt = np.stack([R[c]["shiftp"].transpose(0, 2, 1).reshape(L, 1664) for c in range(8)], axis=1)
    wkv_prompt = np.stack([R[c]["wkvp"].reshape(L, 4, 2, 64, 64).transpose(0, 1, 2, 4, 3).reshape(L, 8, 64, 64) for c in range(8)], axis=1)
    memk_prompt = np.stack([R[c]["memk"].reshape(L, 256, 256).transpose(0, 2, 1).reshape(L, 256, 4, 64) for c in range(8)], axis=1)
    memv_prompt = np.stack([R[c]["memv"].reshape(L, 256, 4, 64) for c in range(8)], axis=1)
    pool_sample = np.concatenate([R[c]["pools"].transpose(0, 2, 3, 1) for c in range(8)], axis=1)
    shift_sample = np.concatenate([R[c]["shifts"].transpose(0, 3, 2, 1).reshape(L, 16, 1664) for c in range(8)], axis=1)
    wkv_sample = np.concatenate([R[c]["wkvs"].reshape(L, 16, 8, 64, 64) for c in range(8)], axis=1)
    outs = (y_prompt, y_sample, pool_prompt, shift_prompt, wkv_prompt, memk_prompt, memv_prompt, pool_sample, shift_sample, wkv_sample)
    outs = tuple(np.ascontiguousarray(o, dtype=np.float32) for o in outs)
    if _trace:
        return outs, res
    return outs
```

```python
import contextlib
import numpy as np
import concourse.bass as bass
import concourse.mybir as mybir
from concourse.bass_utils import run_bass_kernel_spmd

F32 = mybir.dt.float32
BF16 = mybir.dt.bfloat16
AF = mybir.ActivationFunctionType
ALU = mybir.AluOpType
AX = mybir.AxisListType

L_ALL = 4
D = 1024
T = 2048
NS = 16
NT = 512
EPS = 1e-6
GN_EPS = 64 * 1e-5
LWC = 0.6065306597126334
VL = 76
NCST = 1122
NCSTB = 640
SEGW = 1040
G_POOL, G_XA, G_LORA, G_RW = 0, 4, 8, 9
FC_ORDER = [0, 1, 2, 3, 21, 22, 23, 24, 16] + [x for c in range(4) for x in (4 + c, 8 + c, 12 + c, 17 + c)]


DRAM_NAMES = set()
PSUM_NAMES = set()


def _dsize(dt):
    return 2 if dt == BF16 else 4


class Op:
    __slots__ = ("eng", "fn", "deps", "signal", "sigval", "sem", "lane", "id", "row")


class Prog:
    ENGS = ["pe", "act", "dve", "pool", "sp"]

    def __init__(self):
        self.ops = []
        self.recs = {}
        self.lanes = {}
        self.nlane = {"sp": 8, "pool": 8, "act": 4}
        self.rr = {"sp": 0, "pool": 0, "act": 0}

    @staticmethod
    def region(ap):
        a = ap.ap
        pitch, pcnt = a[0]
        off = ap.offset
        ds = _dsize(ap.dtype)
        if ap.name in DRAM_NAMES:
            lo = off
            hi = off + sum(abs(s) * (c - 1) for s, c in a) + 1
            return (ap.name, 0, 1, lo * ds, hi * ds)
        p0 = off // pitch if pitch else 0
        c0 = off - p0 * pitch if pitch else off
        ext = sum(abs(s) * (c - 1) for s, c in a[1:]) + 1
        if ap.name in PSUM_NAMES:
            b0 = (c0 * ds) // 2048 * 2048
            b1 = -((-(c0 + ext) * ds) // 2048) * 2048
            return (ap.name, 0, 128, b0, b1)
        return (ap.name, p0, p0 + pcnt, c0 * ds, (c0 + ext) * ds)

    def add(self, eng, fn, R=(), W=(), dma=False, row=None):
        op = Op()
        op.row = row
        op.eng, op.fn, op.signal, op.sigval, op.sem, op.lane = eng, fn, False, 0, None, None
        op.id = len(self.ops)
        deps = set()
        if dma:
            k = self.rr[eng]
            self.rr[eng] = (k + 1) % self.nlane[eng]
            op.lane = "%s%d" % (eng, k)
            lst = self.lanes.setdefault(op.lane, [])
            if lst:
                deps.add(lst[-1])
            lst.append(op)
        for ap, isw in [(x, False) for x in R] + [(x, True) for x in W]:
            name, p0, p1, b0, b1 = self.region(ap)
            isps = name in PSUM_NAMES
            lst = self.recs.setdefault(name, [])
            keep = []
            for rec in lst:
                q0, q1, c0, c1, o, w = rec
                ov = not (q1 <= p0 or p1 <= q0 or c1 <= b0 or b1 <= c0)
                if ov and isps and (o.eng != eng or (eng == "pe" and o.row != row)):
                    deps.add(o)
                elif ov and (isw or w):
                    same = (o.eng == eng and o.lane is None and not dma)
                    if not (same and eng == "pe"):
                        deps.add(o)
                    elif same and eng != "pe" and w:
                        deps.add(o)
                cov = (isw or isps) and (p0 <= q0 and q1 <= p1 and b0 <= c0 and c1 <= b1)
                dup = (not isw) and (not w) and o.eng == eng and o.lane is None and not dma and (q0, q1, c0, c1) == (p0, p1, b0, b1)
                if not (cov or dup):
                    keep.append(rec)
            keep.append([p0, p1, b0, b1, op, isw])
            self.recs[name] = keep
        deps.discard(op)
        best = {}
        for d in list(deps):
            if d.lane is None:
                if d.eng not in best or best[d.eng].id < d.id:
                    best[d.eng] = d
        deps = set(d for d in deps if d.lane is not None) | set(best.values())
        op.deps = deps
        for d in deps:
            d.signal = True
        self.ops.append(op)
        return op

    def emit(self, nc, block, sems):
        cnt = {}
        for op in self.ops:
            if op.lane is not None:
                op.sem = sems[op.lane]
                cnt[op.lane] = cnt.get(op.lane, 0) + 16
                op.sigval = cnt[op.lane]
            elif op.signal:
                op.sem = sems[op.eng]
                cnt[op.eng] = cnt.get(op.eng, 0) + 1
                op.sigval = cnt[op.eng]
        per = {e: [o for o in self.ops if o.eng == e] for e in self.ENGS}

        def run(eng_name, e):
            known = {}
            for op in per[eng_name]:
                need = {}
                for d in op.deps:
                    if d.sem is None:
                        continue
                    key = id(d.sem)
                    if need.get(key, (None, 0))[1] < d.sigval:
                        need[key] = (d.sem, d.sigval)
                for key, (sem, val) in need.items():
                    if known.get(key, 0) < val:
                        e.wait_ge(sem, val)
                        known[key] = val
                ins = op.fn(e)
                if ins is not None and op.sem is not None:
                    ins.then_inc(op.sem, 16 if op.lane is not None else 1)

        block.tensor(lambda e: run("pe", e))
        block.scalar(lambda e: run("act", e))
        block.vector(lambda e: run("dve", e))
        block.gpsimd(lambda e: run("pool", e))
        block.sync(lambda e: run("sp", e))


def build_program(NL=L_ALL):
    nc = bass.Bass("TRN2", target_bir_lowering=False)
    P = Prog()

    def din(name, shape):
        DRAM_NAMES.add(name)
        return nc.dram_tensor(name, list(shape), F32, kind="ExternalInput").ap()

    def dout(name, shape):
        DRAM_NAMES.add(name)
        return nc.dram_tensor(name, list(shape), F32, kind="ExternalOutput").ap()

    xT = din("xT", [D, T]); xsT = din("xsT", [D, NS]); memT = din("memT", [D, 256])
    spT = din("spT", [L_ALL, 256, NS, 15]); sshT = din("sshT", [L_ALL, 1664, NS]); swkv = din("swkv", [L_ALL, 128, 4096])
    ckT = din("ckT", [L_ALL, NS, 2, 128, 256]); cv = din("cv", [L_ALL, NS, 256, 256])
    win = din("win", [L_ALL, 25, 128, 8, 128]); wout = din("wout", [L_ALL, 8, 128, 8, 128]); wkv = din("wkv", [L_ALL, 128, 8, 512])
    vec = din("vec", [128, VL * L_ALL + 8]); lora2 = din("lora2", [L_ALL, 128, 512]); pwb = din("pwb", [L_ALL, 2, 128, 128])
    cst = din("cst", [128, NCST]); cstb = din("cstb", [128, NCSTB]); bhv = din("bhv", [L_ALL, 128, 192])
    yT = dout("yT", [D, T]); ysT = dout("ysT", [D, NS]); poolp = dout("poolp", [L_ALL, 256, 15]); shiftp = dout("shiftp", [L_ALL, 128, 13])
    wkvp = dout("wkvp", [L_ALL, 4, 128, 64]); memk = dout("memk", [L_ALL, 2, 128, 256]); memv = dout("memv", [L_ALL, 256, 256])
    pools = dout("pools", [L_ALL, 256, NS, 15]); shifts = dout("shifts", [L_ALL, 128, 13, NS]); wkvs = dout("wkvs", [L_ALL, 128, 4096])
    DRAM_NAMES.update(["scr6", "scry"])
    scr6 = nc.dram_tensor("scr6", [NS, 8, 6, 64], F32, kind="Internal").ap()
    scry = nc.dram_tensor("scry", [NS, 512], F32, kind="Internal").ap()

    es = contextlib.ExitStack()
    with es:
        def sb(name, shape, dt):
            return es.enter_context(nc.sbuf_tensor(name, list(shape), dt))

        def ps(name, shape, dt):
            PSUM_NAMES.add(name)
            return es.enter_context(nc.psum_tensor(name, list(shape), dt))

        X = sb("X", [128, 8, T + NS], F32)
        XN = sb("XN", [128, 8, SEGW], BF16)
        MIX = sb("MIX", [128, 8, SEGW], BF16)
        LTS = sb("LTS", [128, SEGW], BF16)
        VEC = sb("VEC", [128, VL * L_ALL + 8], F32)
        CST = sb("CST", [128, NCST], F32)
        CSTB = sb("CSTB", [128, NCSTB], BF16)
        LORA2 = sb("LORA2", [128, 512], BF16)
        PWB = sb("PWB", [128, 2, 128], BF16)
        KTm = sb("KTm", [128, 2, 256], BF16)
        VTm = sb("VTm", [128, 2, 256], BF16)
        ST = sb("ST", [128, 4, 64], F32)
        STb = sb("STb", [128, 4, 64], BF16)
        HALO = sb("HALO", [128, 2, 15], F32)
        PREV = sb("PREV", [128, 13], F32)
        WINB = [sb("WINB%d" % i, [128, 4, 8, 128], BF16) for i in range(2)]
        WOB = [sb("WOB%d" % i, [128, 8, 128], BF16) for i in range(2)]
        NF = 9216
        NB = 17408
        SF = sb("SF", [128, NF], F32)
        SBF = sb("SBF", [128, NB], BF16)
        psA = ps("psA", [128, 1024], F32); psB = ps("psB", [128, 1024], F32)
        psC = ps("psC", [128, 512], F32); psD = ps("psD", [128, 512], F32); psE = ps("psE", [128, 512], F32)
        psT = ps("psT", [128, 1024], BF16)

        class Carve:
            def __init__(s, t, n):
                s.t, s.n, s.o = t, n, 0

            def get(s, cols):
                a = s.t[:, s.o:s.o + cols]
                s.o += cols
                assert s.o <= s.n, (s.o, s.n)
                return a

            def reset(s):
                s.o = 0
        cf = Carve(SF, NF); cb = Carve(SBF, NB)

        ID32 = CST[:, 0:128]; MSU = CST[:, 128:192]; MUU = CST[:, 192:256]; MSL = CST[:, 256:320]
        EYE16 = CST[:, 320:576]; RWIN = CST[:, 576:578]; RCF = CST[:, 578:610]; SEGM = CST[:, 610:1122]
        IDB = CSTB[:, 0:128]; ONESMEAN = CSTB[:, 128:256]; BONES = CSTB[:, 256:384]; BMEAN = CSTB[:, 384:512]
        ID64 = CSTB[:, 512:576]; ONES64 = CSTB[:, 576:640]

        def V(l, off, n=1):
            return VEC[:, l * VL + off: l * VL + off + n]
        O_G, O_MU, O_OMU, O_W0, O_A0, O_KK, O_KA, O_OMKA, O_RK, O_LNG, O_LNB, O_PS, O_MG = 0, 8, 21, 34, 38, 42, 46, 50, 54, 58, 62, 66, 68
        FG = VEC[:, VL * L_ALL: VL * L_ALL + 8]

        def dma(q, out, in_):
            P.add(q, lambda e: e.dma_start(out=out, in_=in_), R=[in_], W=[out], dma=True)

        def mm(out, lhsT, rhs, start=True, stop=True, tp=None, sgc=False):
            kw = {"skip_group_check": True} if sgc else {}
            if tp is None:
                P.add("pe", lambda e: e.matmul(out, lhsT, rhs, start=start, stop=stop, **kw), R=[lhsT, rhs], W=[out], row=-1)
            else:
                P.add("pe", lambda e: e.matmul(out, lhsT, rhs, start=start, stop=stop, tile_position=tp, **kw), R=[lhsT, rhs], W=[out], row=tp[0])

        def tr(out, in_, ident):
            P.add("pe", lambda e: e.transpose(out, in_, ident), R=[in_, ident], W=[out], row=-1)

        def act(out, in_, func, bias=None, scale=None):
            kw = {}
            R = [in_]
            if bias is not None:
                kw["bias"] = bias
                if not isinstance(bias, float):
                    R.append(bias)
            if scale is not None:
                kw["scale"] = scale
                if not isinstance(scale, float):
                    R.append(scale)
            P.add("act", lambda e: e.activation(out=out, in_=in_, func=func, **kw), R=R, W=[out])

        def tt(eng, out, a, b, op):
            P.add(eng, lambda e: e.tensor_tensor(out=out, in0=a, in1=b, op=op), R=[a, b], W=[out])

        def ts(eng, out, a, s1, op0, s2=None, op1=None):
            R = [a] + [s for s in (s1, s2) if s is not None and not isinstance(s, float)]
            if op1 is None:
                P.add(eng, lambda e: e.tensor_scalar(out=out, in0=a, scalar1=s1, scalar2=None, op0=op0), R=R, W=[out])
            else:
                P.add(eng, lambda e: e.tensor_scalar(out=out, in0=a, scalar1=s1, scalar2=s2, op0=op0, op1=op1), R=R, W=[out])

        def stt(eng, out, a, s, b, op0, op1):
            eng = "dve"
            R = [a, b] + ([] if isinstance(s, float) else [s])
            P.add(eng, lambda e: e.scalar_tensor_tensor(out=out, in0=a, scalar=s, in1=b, op0=op0, op1=op1), R=R, W=[out])

        def cp(eng, out, in_):
            if eng == "act":
                act(out, in_, AF.Copy)
            else:
                P.add(eng, lambda e: e.tensor_copy(out=out, in_=in_), R=[in_], W=[out])

        def recip(out, in_):
            P.add("dve", lambda e: e.reciprocal(out=out, in_=in_), R=[in_], W=[out])

        def rsum(out, in_):
            P.add("dve", lambda e: e.reduce_sum(out=out, in_=in_, axis=AX.X), R=[in_], W=[out])

        def rmax(out, in_):
            P.add("dve", lambda e: e.reduce_max(out=out, in_=in_, axis=AX.X), R=[in_], W=[out])

        def memset(eng, out, val):
            P.add(eng, lambda e: e.memset(out, val), W=[out])

        ew_rr = [0]

        def EW():
            ew_rr[0] ^= 1
            return "dve" if ew_rr[0] else "pool"

        dma("sp", VEC[:], vec); dma("sp", CST[:], cst); dma("pool", CSTB[:], cstb)
        for kc in range(8):
            dma("sp", X[:, kc, 0:T], xT[kc * 128:(kc + 1) * 128, :])
        dma("sp", X[:, :, T:T + NS], xsT.rearrange("(k p) b -> p k b", p=128))
        for l in range(NL):
            ts("dve", V(l, O_OMU, 13), V(l, O_MU, 13), -1.0, ALU.mult, 1.0, ALU.add)
            ts("dve", V(l, O_OMKA, 4), V(l, O_KA, 4), -1.0, ALU.mult, 1.0, ALU.add)

        def rms_rstd(srcf, W, rs_out, sq_tmp, pst):
            for kc in range(8):
                sq = sq_tmp[kc % 2]
                act(sq[:, :W], srcf(kc), AF.Square)
                mm(pst[:, :W], ONESMEAN, sq[:, :W], start=(kc == 0), stop=(kc == 7))
            act(rs_out, pst[:, :W], AF.Sqrt, bias=EPSC[:, 0:1])
            recip(rs_out, rs_out)

        EPSC = sb("EPSC", [128, 4], F32)
        memset("dve", EPSC[:, 0:1], EPS); memset("dve", EPSC[:, 1:2], GN_EPS); memset("dve", EPSC[:, 2:3], 0.0)

        def mem_stage(l):
            cf.reset(); cb.reset()
            MEMX = cf.get(8 * 256).rearrange("p (k m) -> p k m", k=8)
            RSM = cf.get(256); KO = cf.get(256); VO = cf.get(256)
            SQT = [cb.get(512), cb.get(512)]
            MN = cb.get(8 * 256).rearrange("p (k m) -> p k m", k=8)
            WKVB = cb.get(8 * 512).rearrange("p (k f) -> p k f", k=8)
            dma("sp", MEMX, memT.rearrange("(k p) m -> p k m", p=128))
            dma("pool", WKVB, wkv[l])
            rms_rstd(lambda kc: MEMX[:, kc, :], 256, RSM, SQT, psC)
            for kc in range(8):
                stt("dve", MN[:, kc, :], MEMX[:, kc, :], V(l, O_MG + kc), RSM, ALU.mult, ALU.mult)
            for fc in range(2):
                for kc in range(8):
                    mm(psA[:, fc * 512: fc * 512 + 256], WKVB[:, kc, fc * 128:(fc + 1) * 128], MN[:, kc, :], start=(kc == 0), stop=(kc == 7))
                cp("act", KTm[:, fc, :], psA[:, fc * 512: fc * 512 + 256])
                cp("dve", KO, psA[:, fc * 512: fc * 512 + 256])
                dma("sp", memk[l, fc], KO)
            for mt in range(2):
                for kc in range(8):
                    mm(psB[:, mt * 512: mt * 512 + 256], MN[:, kc, mt * 128:(mt + 1) * 128], WKVB[:, kc, 256:512], start=(kc == 0), stop=(kc == 7))
                cp("act", VTm[:, mt, :], psB[:, mt * 512: mt * 512 + 256])
                cp("dve", VO, psB[:, mt * 512: mt * 512 + 256])
                dma("sp", memv[l, mt * 128:(mt + 1) * 128, :], VO)

        P.marks = []

        def mark(nm):
            P.marks.append((nm, len(P.ops)))
        mark("mem_done")
        SEGS = [[("P", 0), ("P", 1)], [("P", 2), ("P", 3), ("S", 0)]]

        def tiles_of(seg):
            out = []
            lc = 0
            for kind, i in seg:
                W = NT if kind == "P" else NS
                gx = i * NT if kind == "P" else T
                out.append(dict(kind=kind, i=i, W=W, gx=gx, lc=lc))
                lc += W
            return out

        wq = [0]

        GROUPS = [("POOL", G_POOL, 4), ("XA", G_XA, 4), ("LORA", G_LORA, 1)] + [("RW%d" % c_, G_RW + 4 * c_, 4) for c_ in range(4)]
        plan = [(l_, s_, g_) for l_ in range(NL) for s_ in range(2) for g_ in range(len(GROUPS))]
        loaded = {}

        def get_w(l_, s_, g_):
            i0 = plan.index((l_, s_, g_))
            for i in (i0, i0 + 1):
                if i < len(plan) and plan[i] not in loaded:
                    pl, ps_, pg = plan[i]
                    loaded[plan[i]] = load_win(pl, GROUPS[pg][1], GROUPS[pg][2])
            return loaded[(l_, s_, g_)]

        def load_win(l, pos, n):
            wb = WINB[wq[0] % 2]
            wq[0] += 1
            dma("pool", wb[:, 0:n], win[l, pos:pos + n].rearrange("c p k f -> p c k f"))
            return wb

        def win_mm(wb, i, tl, out):
            W, lc = tl["W"], tl["lc"]
            for kc in range(8):
                mm(out[:, :W], wb[:, i, kc, :], XN[:, kc, lc:lc + W], start=(kc == 0), stop=(kc == 7))

        def tshift(l, j, pst, tl, xs, a1, SSH=None, SHS=None):
            W = tl["W"]
            act(a1[:, :W], pst[:, :W], AF.Identity, scale=V(l, O_OMU + j))
            if tl["kind"] == "P":
                stt("dve", xs[:, 1:W], pst[:, 0:W - 1], V(l, O_MU + j), a1[:, 1:W], ALU.mult, ALU.add)
                stt("dve", xs[:, 0:1], PREV[:, j:j + 1], V(l, O_MU + j), a1[:, 0:1], ALU.mult, ALU.add)
                cp("act", PREV[:, j:j + 1], pst[:, W - 1:W])
            else:
                stt("dve", xs[:, :W], SSH[:, j, :], V(l, O_MU + j), a1[:, :W], ALU.mult, ALU.add)
                cp("act", SHS[:, j, :], pst[:, :W])

        def norm_gen(l_, tls_):
            cf.reset(); cb.reset()
            SQT_ = [cb.get(512), cb.get(512)]
            RS_ = cf.get(SEGW)
            for tl_ in tls_:
                W, gx, lc = tl_["W"], tl_["gx"], tl_["lc"]
                rms_rstd(lambda kc: X[:, kc, gx:gx + W], W, RS_[:, lc:lc + W], SQT_, psC)
                yield
                for kc in range(8):
                    stt("dve", XN[:, kc, lc:lc + W], X[:, kc, gx:gx + W], V(l_, O_G + kc), RS_[:, lc:lc + W], ALU.mult, ALU.mult)
                    if kc % 2 == 1:
                        yield

        for l in range(NL):
            dma("pool", LORA2[:], lora2[l]); dma("pool", PWB[:], pwb[l].rearrange("c p f -> p c f"))
            mem_stage(l)
            memset("dve", HALO[:], 0.0); memset("dve", PREV[:], 0.0); memset("dve", ST[:], 0.0); memset("dve", STb[:], 0.0)
            for si, seg in enumerate(SEGS):
                tls = tiles_of(seg)
                has_s = any(t["kind"] == "S" for t in tls)
                dma("pool", WOB[0][:], wout[l, 0]); dma("pool", WOB[1][:], wout[l, 1])
                if l == 0 and si == 0:
                    for _ in norm_gen(l, tls):
                        pass
                mark("norm_done")
                cf.reset(); cb.reset()
                wb = get_w(l, si, 0)
                PVB = cf.get(2 * 527).rearrange("p (c t) -> p c t", c=2)
                SGP = cf.get(1024).rearrange("p (c t) -> p c t", c=2)
                W2 = cf.get(2 * 526).rearrange("p (c t) -> p c t", c=2)
                W4 = cf.get(2 * 524).rearrange("p (c t) -> p c t", c=2)
                W8 = cf.get(520); W16 = cf.get(512)
                TMPP = cf.get(32).rearrange("p (c t) -> p c t", c=2)
                PL = cb.get(1024).rearrange("p (c t) -> p c t", c=2)
                SP = cf.get(2 * 16 * 16).rearrange("p (c b r) -> p c b r", c=2, b=16)
                W2s = cf.get(2 * 16 * 15).rearrange("p (c b r) -> p c b r", c=2, b=16)
                W4s = cf.get(2 * 16 * 13).rearrange("p (c b r) -> p c b r", c=2, b=16)
                W8s = cf.get(16 * 9).rearrange("p (b r) -> p b r", b=16)
                W16s = cf.get(16).rearrange("p (b r) -> p b r", b=16)
                for tl in tls:
                    W, gx, lc = tl["W"], tl["gx"], tl["lc"]
                    outs = [psA[:, 0:512], psA[:, 512:1024], psB[:, 0:512], psB[:, 512:1024]]
                    for i in range(4):
                        win_mm(wb, i, tl, outs[i])
                    if tl["kind"] == "P":
                        cp("pool", PVB[:, :, 0:15], HALO[:])
                        for ch in range(2):
                            cp("act", PVB[:, ch, 15:15 + W], outs[ch][:, :W])
                            act(SGP[:, ch, :W], outs[2 + ch][:, :W], AF.Silu)
                        cp("pool", HALO[:], PVB[:, :, W:W + 15])
                        if tl["i"] == 3:
                            dma("sp", poolp[l].rearrange("(c p) r -> p c r", p=128), PVB[:, :, W:W + 15])
                        tt("dve", W2[:, :, 0:W + 14], PVB[:, :, 1:W + 15], PVB[:, :, 0:W + 14], ALU.add)
                        tt("pool", W4[:, :, 0:W + 12], W2[:, :, 2:W + 14], W2[:, :, 0:W + 12], ALU.add)
                        tt("dve", W8[:, 0:W + 8], W4[:, 1, 4:W + 12], W4[:, 1, 0:W + 8], ALU.add)
                        tt("pool", W16[64:128, 0:W], W8[64:128, 8:W + 8], W8[64:128, 0:W], ALU.add)
                        wsum = [(0, 0, 64, W2[0:64, 0, 14:14 + W]), (0, 64, 128, W4[64:128, 0, 12:12 + W]),
                                (1, 0, 64, W8[0:64, 8:8 + W]), (1, 64, 128, W16[64:128, 0:W])]
                        for ch, p0, p1, ws in wsum:
                            stt("dve", PL[p0:p1, ch, :W], ws, RWIN[p0:p1, ch:ch + 1], PVB[p0:p1, ch, 15:15 + W], ALU.mult, ALU.subtract)
                            if tl["i"] == 0:
                                tt("dve", TMPP[p0:p1, ch, 0:15], ws[:, 0:15], RCF[p0:p1, ch * 16:ch * 16 + 15], ALU.mult)
                                tt("dve", PL[p0:p1, ch, 0:15], TMPP[p0:p1, ch, 0:15], PVB[p0:p1, ch, 15:30], ALU.subtract)
                        pl = [PL[:, 0, :W], PL[:, 1, :W]]
                        sg = [SGP[:, 0, :W], SGP[:, 1, :W]]
                    else:
                        for ch in range(2):
                            dma("sp", SP[:, ch, :, 0:15], spT[l][ch * 128:(ch + 1) * 128])
                        for ch in range(2):
                            cp("act", SP[:, ch, :, 15], outs[ch][:, :W])
                            act(SGP[:, ch, :W], outs[2 + ch][:, :W], AF.Silu)
                        for ch in range(2):
                            dma("sp", pools[l][ch * 128:(ch + 1) * 128], SP[:, ch, :, 1:16])
                        tt("dve", W2s[:], SP[:, :, :, 1:16], SP[:, :, :, 0:15], ALU.add)
                        tt("dve", W4s[:], W2s[:, :, :, 2:15], W2s[:, :, :, 0:13], ALU.add)
                        tt("dve", W8s[:], W4s[:, 1, :, 4:13], W4s[:, 1, :, 0:9], ALU.add)
                        tt("dve", W16s[64:128], W8s[64:128, :, 8:9], W8s[64:128, :, 0:1], ALU.add)
                        wsum = [(0, 0, 64, W2s[0:64, 0, :, 14]), (0, 64, 128, W4s[64:128, 0, :, 12]),
                                (1, 0, 64, W8s[0:64, :, 8]), (1, 64, 128, W16s[64:128, :, 0])]
                        for ch, p0, p1, ws in wsum:
                            stt("dve", PL[p0:p1, ch, :W], ws, RWIN[p0:p1, ch:ch + 1], SP[p0:p1, ch, :, 15], ALU.mult, ALU.subtract)
                        pl = [PL[:, 0, :W], PL[:, 1, :W]]
                        sg = [SGP[:, 0, :W], SGP[:, 1, :W]]
                    for ch in range(2):
                        pso = psC if ch == 0 else psD
                        mm(pso[:, :W], PWB[:, ch, :], pl[ch])
                        stt("dve", MIX[:, ch, lc:lc + W], pso[:, :W], V(l, O_PS + ch), sg[ch], ALU.mult, ALU.mult)
                mark("pool_done")
                cf.reset(); cb.reset()
                wb = get_w(l, si, 1)
                QB = cb.get(1024).rearrange("p (c t) -> p c t", c=2)
                ET2 = [cb.get(1024).rearrange("p (c t) -> p c t", c=2), cb.get(1024).rearrange("p (c t) -> p c t", c=2)]
                KC2 = [cb.get(4 * 512).rearrange("p (b c m) -> p b c m", b=4, c=2) for _p in range(2)]
                VC2 = [cb.get(4 * 512).rearrange("p (b t f) -> p b t f", b=4, t=2) for _p in range(2)]

                def loadK(bg):
                    for bb in range(4):
                        dma("pool", KC2[bg % 2][:, bb], ckT[l, bg * 4 + bb].rearrange("c p m -> p c m"))

                def loadV(bg):
                    for bb in range(4):
                        dma("pool", VC2[bg % 2][:, bb], cv[l, bg * 4 + bb].rearrange("(t p) f -> p t f", p=128))
                if has_s:
                    loadK(0); loadV(0)
                SGX = cf.get(1024).rearrange("p (c t) -> p c t", c=2)
                RD = cf.get(512); TO = cf.get(512)
                for tl in tls:
                    W, gx, lc = tl["W"], tl["gx"], tl["lc"]
                    outs = [psA[:, 0:512], psA[:, 512:1024], psB[:, 0:512], psB[:, 512:1024]]
                    for i in range(4):
                        win_mm(wb, i, tl, outs[i])
                    for ch in range(2):
                        cp("act", QB[:, ch, :W], outs[ch][:, :W])
                        act(SGX[:, ch, :W], outs[2 + ch][:, :W], AF.Silu)
                    if tl["kind"] == "P":
                        scb = [(psC, psD), (psB[:, 0:512], psB[:, 512:1024])]

                        def xa_S(h):
                            hc, e = h // 2, h % 2
                            pe_ = slice(64 * e, 64 * e + 64)
                            for mt in range(2):
                                mm(scb[h % 2][mt][:, :W], KTm[pe_, hc, mt * 128:(mt + 1) * 128], QB[pe_, hc, :W], tp=(64 * e, 0))

                        def xa_E(h):
                            for mt in range(2):
                                act(ET2[h % 2][:, mt, :W], scb[h % 2][mt][:, :W], AF.Exp, scale=0.125)

                        def xa_PVD(h):
                            hc, e = h // 2, h % 2
                            pe_ = slice(64 * e, 64 * e + 64)
                            for mt in range(2):
                                mm(psE[pe_, :W], VTm[:, mt, h * 64:(h + 1) * 64], ET2[h % 2][:, mt, :W], start=(mt == 0), stop=(mt == 1), tp=(0, 64 * e))
                            for mt in range(2):
                                mm(psA[pe_, :W], ONES64, ET2[h % 2][:, mt, :W], start=(mt == 0), stop=(mt == 1), tp=(0, 64 * e))

                        xa_S(0); xa_S(1)
                        for h in range(4):
                            xa_E(h)
                            xa_PVD(h)
                            if h + 2 < 4:
                                xa_S(h + 2)
                            if h % 2 == 1:
                                hc = h // 2
                                recip(RD[:, :W], psA[:, :W])
                                tt("dve", TO[:, :W], psE[:, :W], RD[:, :W], ALU.mult)
                                tt("pool", MIX[:, 6 + hc, lc:lc + W], TO[:, :W], SGX[:, hc, :W], ALU.mult)
                    else:
                        QM = cb.get(2 * 256).rearrange("p (c b t) -> p c b t", c=2, b=16)
                        PTM = cb.get(8 * 256).rearrange("p (x b t) -> p x b t", x=8, b=16)
                        PS_ = cf.get(1024).rearrange("p (h m) -> p h m", h=4)
                        MXs = cf.get(4); NBs = cf.get(4); SMs = cf.get(4); OS = cf.get(256)
                        E16 = EYE16.rearrange("p (b t) -> p b t", b=16)
                        for hc in range(2):
                            tt("dve", QM[:, hc], QB[:, hc, 0:16].unsqueeze(1).to_broadcast([128, 16, 16]), E16, ALU.mult)
                        for bg in range(4):
                            KC = KC2[bg % 2]
                            if bg + 1 < 4:
                                loadK(bg + 1)
                            for h in (0, 2, 1, 3):
                                hc, e = h // 2, h % 2
                                pe_ = slice(64 * e, 64 * e + 64)
                                for bb in range(4):
                                    b = bg * 4 + bb
                                    mm(psA[0:16, h * 256:(h + 1) * 256], QM[pe_, hc, b, :], KC[pe_, bb, hc, :], start=(b == 0 and h % 2 == 0), stop=(b == 15), tp=(64 * e, 0), sgc=True)
                        SCV = psA[0:16, :].rearrange("p (h m) -> p h m", h=4)
                        rmax(MXs[0:16, :], SCV)
                        ts("dve", NBs[0:16, :], MXs[0:16, :], -0.125, ALU.mult)
                        for h in range(4):
                            act(PS_[0:16, h, :], SCV[:, h, :], AF.Exp, bias=NBs[0:16, h:h + 1], scale=0.125)
                        rsum(SMs[0:16, :], PS_[0:16])
                        for h in range(4):
                            for mt in range(2):
                                tr(psC[:, (h * 2 + mt) * 16:(h * 2 + mt) * 16 + 16], PS_[0:16, h, mt * 128:(mt + 1) * 128], ID32[0:16, 0:16])
                        tt("dve", PTM[:], psC[:, 0:128].rearrange("p (x t) -> p x t", x=8).unsqueeze(2).to_broadcast([128, 8, 16, 16]),
                           E16.unsqueeze(1).to_broadcast([128, 8, 16, 16]), ALU.mult)
                        for bg in range(4):
                            VC = VC2[bg % 2]
                            if bg + 1 < 4:
                                loadV(bg + 1)
                            for bb in range(4):
                                b = bg * 4 + bb
                                for h in range(4):
                                    for mt in range(2):
                                        mm(psD[0:16, h * 64:(h + 1) * 64], PTM[:, h * 2 + mt, b, :], VC[:, bb, mt, h * 64:(h + 1) * 64],
                                           start=(b == 0 and mt == 0 and h == 0), stop=(b == 15 and mt == 1), sgc=True)
                        recip(SMs[0:16, :], SMs[0:16, :])
                        tt("dve", OS[0:16, :].rearrange("p (h d) -> p h d", h=4), psD[0:16, 0:256].rearrange("p (h d) -> p h d", h=4),
                           SMs[0:16, :].unsqueeze(2).to_broadcast([16, 4, 64]), ALU.mult)
                        for hc in range(2):
                            tr(psE[:, hc * 16:(hc + 1) * 16], OS[0:16, hc * 128:(hc + 1) * 128], ID32[0:16, 0:16])
                            tt("dve", MIX[:, 6 + hc, lc:lc + W], psE[:, hc * 16:(hc + 1) * 16], SGX[:, hc, :W], ALU.mult)
                mark("xa_done")
                cf.reset(); cb.reset()
                wb = get_w(l, si, 2)
                XSL = cf.get(512); A1 = cf.get(512)
                SSH = cf.get(13 * 16).rearrange("p (j b) -> p j b", j=13)
                SHS = cf.get(13 * 16).rearrange("p (j b) -> p j b", j=13)
                SV6 = cf.get(4 * 6 * 16).rearrange("p (c v b) -> p c v b", c=4, v=6)
                SGGS = cf.get(4 * 16).rearrange("p (c b) -> p c b", c=4)
                if has_s:
                    dma("sp", SSH, sshT[l].rearrange("(j p) b -> p j b", p=128))
                for tl in tls:
                    W, lc = tl["W"], tl["lc"]
                    win_mm(wb, 0, tl, psC)
                    tshift(l, 12, psC, tl, XSL, A1, SSH, SHS)
                    act(LTS[0:64, lc:lc + W], XSL[0:64, :W], AF.Tanh)
                    cp("dve", LTS[64:128, lc:lc + W], XSL[64:128, :W])
                mark("lora_done")
                f_mark, b_mark = cf.o, cb.o
                XR = XSL
                XK = cf.get(512); XV = cf.get(512)
                SG = cf.get(512); AA = cf.get(512); RN = cf.get(512); KKN = cf.get(512); KP = cf.get(512); KA = cf.get(512)
                ENW = cf.get(512); EWt = cf.get(512)
                CUM = RN
                ECW = RN
                YB2 = [cf.get(512), cf.get(512)]
                HS2 = [cf.get(512).rearrange("p (n i) -> p n i", n=8), cf.get(512).rearrange("p (n i) -> p n i", n=8)]
                WC2 = [cf.get(8), cf.get(8)]
                TMPS = cf.get(64)
                T1 = SG
                T2 = AA
                YS = KKN
                KK2 = cb.get(512); RKb = KK2
                BT = cb.get(512); KT_ = cb.get(512); BH = cb.get(512); KH = cb.get(512); VB = cb.get(512)
                MPa_f = cb.get(1024)
                MPa = MPa_f.rearrange("p (x t) -> p x t", x=8)
                MPb = cb.get(1024).rearrange("p (x t) -> p x t", x=8)
                La = cb.get(512).rearrange("p (x t) -> p x t", x=8)
                Lb = cb.get(512).rearrange("p (x t) -> p x t", x=8)
                YBF = MPa_f[:, 0:512]; YSQ = MPa_f[:, 512:1024]
                AR = cb.get(1024).rearrange("p (n x) -> p n x", n=8)
                ATc = cb.get(512)
                VT = cb.get(512).rearrange("p (r f) -> p r f", r=4)
                KHT = cb.get(512).rearrange("p (r f) -> p r f", r=4)
                BHT = cb.get(512).rearrange("p (r f) -> p r f", r=4)
                ATT = cb.get(512).rearrange("p (r f) -> p r f", r=4)
                ATAK = cb.get(512).rearrange("p (x t) -> p x t", x=8)
                ATRB = cb.get(512).rearrange("p (x t) -> p x t", x=8)
                ATRK = cb.get(512).rearrange("p (x t) -> p x t", x=8)
                PTF = cb.get(512).rearrange("p (x t) -> p x t", x=8)
                X0b = cb.get(512).rearrange("p (x t) -> p x t", x=8)
                VPb = cb.get(512).rearrange("p (x t) -> p x t", x=8)
                APb = cb.get(512).rearrange("p (x t) -> p x t", x=8)
                G0T2 = [cb.get(512).rearrange("p (n j) -> p n j", n=8) for _p in range(2)]
                RH2 = [cb.get(512).rearrange("p (n t) -> p n t", n=8) for _p in range(2)]
                Y02 = [cb.get(512), cb.get(512)]
                SGG2 = [cb.get(512), cb.get(512)]
                units = [(c, tl) for c in range(4) for tl in tls]
                wbs = {}
                ppar = {}
                for _ui, (_c, _tl) in enumerate(units):
                    ppar[_ui] = sum(1 for (_c2, _t2) in units[:_ui] if _t2["kind"] == "P") % 2

                def front(ui):
                    c, tl = units[ui]
                    par = ppar[ui]
                    SGG, YB, HS, WC = SGG2[par], YB2[par], HS2[par], WC2[par]
                    G0T, RH, Y0 = G0T2[par], RH2[par], Y02[par]
                    W, gx, lc = tl["W"], tl["gx"], tl["lc"]
                    isP = tl["kind"] == "P"
                    if c not in wbs:
                        wbs[c] = get_w(l, si, 3 + c)
                    wb = wbs[c]
                    outs = [psA[:, 0:512], psA[:, 512:1024], psB[:, 0:512], psB[:, 512:1024]]
                    mark("u_start")
                    for i in range(4):
                        win_mm(wb, i, tl, outs[i])
                    yield
                    tshift(l, 4 + c, outs[1], tl, XK, A1, SSH, SHS)
                    act(KK2[:, :W], XK[:, :W], AF.Square, scale=V(l, O_KK + c))
                    tshift(l, 8 + c, outs[2], tl, XV, A1, SSH, SHS)
                    act(SGG[:, :W] if isP else SGGS[:, c, :], outs[3][:, :W], AF.Silu)
                    tshift(l, c, outs[0], tl, XR, A1, SSH, SHS)
                    yield
                    mm(psB[:, 0:W], LORA2[0:64, c * 128:(c + 1) * 128], LTS[0:64, lc:lc + W], tp=(0, 0))
                    mm(psB[:, 512:512 + W], LORA2[64:128, c * 128:(c + 1) * 128], LTS[64:128, lc:lc + W], tp=(64, 0))
                    mm(psC[:, :W], BONES, KK2[:, :W])
                    act(SG[:, :W], psB[:, 0:W], AF.Sigmoid, bias=V(l, O_W0 + c))
                    ts("dve", RN[:, :W], psC[:, :W], 1e-24, ALU.max)
                    act(AA[:, :W], psB[:, 512:512 + W], AF.Sigmoid, bias=V(l, O_A0 + c))
                    act(RN[:, :W], RN[:, :W], AF.Sqrt)
                    recip(RN[:, :W], RN[:, :W])
                    act(KP[:, :W], AA[:, :W], AF.Identity, bias=V(l, O_OMKA + c), scale=V(l, O_KA + c))
                    stt("dve", KKN[:, :W], XK[:, :W], V(l, O_KK + c), RN[:, :W], ALU.mult, ALU.mult)
                    tt("pool", KP[:, :W], KP[:, :W], XK[:, :W], ALU.mult)
                    tt("pool", KA[:, :W], KKN[:, :W], AA[:, :W], ALU.mult)
                    if not isP:
                        cp("pool", SV6[:, c, 0, :], XR[:, :W])
                        act(SV6[:, c, 1, :], SG[:, :W], AF.Exp, scale=-LWC)
                        cp("pool", SV6[:, c, 2, :], KP[:, :W])
                        cp("pool", SV6[:, c, 3, :], XV[:, :W])
                        cp("pool", SV6[:, c, 4, :], KKN[:, :W])
                        cp("pool", SV6[:, c, 5, :], KA[:, :W])
                        yield
                        return
                    P.add("dve", lambda e, o=CUM[:, :W], m=SEGM[:, :W], d=SG[:, :W]: e.tensor_tensor_scan(out=o, data0=m, data1=d, initial=0.0, op0=ALU.mult, op1=ALU.add),
                          R=[SEGM[:, :W], SG[:, :W]], W=[CUM[:, :W]])
                    stt("dve", RKb[:, :W], XR[:, :W], V(l, O_RK + c), KP[:, :W], ALU.mult, ALU.mult)
                    act(EWt[:, :W], CUM[:, :W], AF.Exp, scale=-LWC)
                    act(ENW[:, :W], CUM[:, :W], AF.Exp, scale=LWC)
                    yield
                    mm(psC[:, :W], BONES, RKb[:, :W])
                    v3 = lambda a_: a_[:, :W].rearrange("p (n s) -> p n s", s=64)
                    tt("dve", v3(ECW), v3(ENW), v3(EWt)[:, :, 63:64].to_broadcast([128, 8, 64]), ALU.mult)
                    cp("pool", WC[:, 0:8], v3(EWt)[:, :, 63])
                    tt("pool", BT[:, :W], KA[:, :W], ENW[:, :W], ALU.mult)
                    tt("pool", KT_[:, :W], KP[:, :W], ENW[:, :W], ALU.mult)
                    tt("pool", AR[:, :, 64:128], v3(XR), v3(EWt), ALU.mult)
                    stt("dve", AR[:, :, 1:64], v3(KKN)[:, :, 1:64], -1.0, v3(EWt)[:, :, 0:63], ALU.mult, ALU.mult)
                    ts("pool", AR[:, :, 0:1], v3(KKN)[:, :, 0:1], -1.0, ALU.mult)
                    cp("pool", VB[:, :W], XV[:, :W])
                    tt("dve", YB[:, :W], psC[:, :W], XV[:, :W], ALU.mult)
                    tt("pool", KH[:, :W], KP[:, :W], ECW[:, :W], ALU.mult)
                    tt("pool", BH[:, :W], KA[:, :W], ECW[:, :W], ALU.mult)
                    cp("pool", v3(ATc), AR[:, :, 0:64])
                    yield
                    for e in range(2):
                        pe_ = slice(64 * e, 64 * e + 64)
                        for n in range(8):
                            q, r = n % 2, n // 2
                            pq = slice(64 * q, 64 * q + 64)
                            x = r * 2 + e
                            tok = slice(n * 64, (n + 1) * 64)
                            mm(psA[pq, x * 128:(x + 1) * 128], BT[pe_, tok], AR[pe_, n, :], tp=(64 * e, 64 * q))
                            mm(psB[pq, x * 128:(x + 1) * 128], KT_[pe_, tok], AR[pe_, n, :], tp=(64 * e, 64 * q))
                            mm(psC[pq, x * 64:(x + 1) * 64], AR[pe_, n, 0:64], BT[pe_, tok], tp=(64 * e, 64 * q))
                    for src, half in ((VB, 0), (KH, 1)):
                        for r in range(4):
                            tr(psT[:, half * 512 + r * 128: half * 512 + (r + 1) * 128], src[:, r * 128:(r + 1) * 128], IDB)
                    A8 = psA[:, :].rearrange("p (x t) -> p x t", x=8)
                    B8 = psB[:, :].rearrange("p (x t) -> p x t", x=8)
                    C8 = psC[:, :].rearrange("p (x t) -> p x t", x=8)
                    bc = lambda m: m.unsqueeze(1).to_broadcast([128, 8, 64])
                    tt("dve", MPa[:, :, 0:64], A8[:, :, 0:64], bc(MSU), ALU.mult)
                    tt("dve", La[:], C8, bc(MSL), ALU.mult)
                    tt("pool", MPa[:, :, 64:128], MPa[:, :, 0:64], bc(ID64), ALU.add)
                    cp("act", VT[:], psT[:, 0:512].rearrange("p (r f) -> p r f", r=4))
                    cp("act", KHT[:], psT[:, 512:1024].rearrange("p (r f) -> p r f", r=4))
                    tt("dve", ATRB[:], A8[:, :, 64:128], bc(MUU), ALU.mult)
                    tt("dve", ATAK[:], B8[:, :, 0:64], bc(MSU), ALU.mult)
                    tt("dve", ATRK[:], B8[:, :, 64:128], bc(MUU), ALU.mult)
                    yield
                    MPc, MPn, Lc, Ln = MPa, MPb, La, Lb
                    for k in range(6):
                        PSM = psA if k % 2 == 0 else psB
                        LO = psB if k % 2 == 0 else psA
                        M8 = PSM[:, :].rearrange("p (x t) -> p x t", x=8)
                        L8 = LO[:, :].rearrange("p (x t) -> p x t", x=8)
                        for xh in range(2):
                            xs = slice(xh * 4, xh * 4 + 4)
                            for q in ((0, 1) if (k + xh) % 2 == 0 else (1, 0)):
                                pq = slice(64 * q, 64 * q + 64)
                                tpq = (64 * q, 64 * q)
                                for x in range(xh * 4, xh * 4 + 4):
                                    if k == 0:
                                        mm(PSM[pq, x * 128:x * 128 + 64], Lc[pq, x, :], MPc[pq, x, 0:64], tp=tpq)
                                    elif k < 5:
                                        mm(PSM[pq, x * 128:(x + 1) * 128], Lc[pq, x, :], MPc[pq, x, :], tp=tpq)
                                    else:
                                        mm(PSM[pq, x * 128:x * 128 + 64], Lc[pq, x, :], MPc[pq, x, 64:128], tp=tpq)
                                if k < 5:
                                    for x in range(xh * 4, xh * 4 + 4):
                                        mm(LO[pq, x * 128:x * 128 + 64], MPc[pq, x, 0:64], Lc[pq, x, :], tp=tpq)
                            if k == 0:
                                cp("act", MPn[:, xs, 0:64], M8[:, xs, 0:64])
                                cp("pool", MPn[:, xs, 64:128], MPc[:, xs, 64:128])
                                cp("act", Ln[:, xs, :], L8[:, xs, 0:64])
                            elif k < 5:
                                cp("act", MPn[:, xs, 0:64], M8[:, xs, 0:64])
                                tt("dve", MPn[:, xs, 64:128], M8[:, xs, 64:128], MPc[:, xs, 64:128], ALU.add)
                                cp("act", Ln[:, xs, :], L8[:, xs, 0:64])
                            else:
                                tt("dve", PTF[:, xs, :], M8[:, xs, 0:64], MPc[:, xs, 64:128], ALU.add)
                        if k == 0:
                            for r in range(4):
                                tr(psT[:, r * 128:(r + 1) * 128], BH[:, r * 128:(r + 1) * 128], IDB)
                                tr(psT[:, 512 + r * 128:512 + (r + 1) * 128], ATc[:, r * 128:(r + 1) * 128], IDB)
                            cp("act", BHT[:], psT[:, 0:512].rearrange("p (r f) -> p r f", r=4))
                            cp("act", ATT[:], psT[:, 512:1024].rearrange("p (r f) -> p r f", r=4))
                        if k == 1:
                            for q in range(2):
                                pq = slice(64 * q, 64 * q + 64)
                                for r in range(4):
                                    for e in range(2):
                                        x = r * 2 + e
                                        mm(psC[pq, x * 64:(x + 1) * 64], ATAK[pq, x, :], VT[pq, r, e * 64:(e + 1) * 64], tp=(64 * q, 64 * q))
                            cp("act", X0b[:], psC[:, :].rearrange("p (x t) -> p x t", x=8))
                        MPc, MPn, Lc, Ln = MPn, MPc, Ln, Lc
                        yield
                    for q in range(2):
                        pq = slice(64 * q, 64 * q + 64)
                        for r in range(4):
                            for e in range(2):
                                x = r * 2 + e
                                mm(psA[pq, x * 64:(x + 1) * 64], PTF[pq, x, :], X0b[pq, x, :], tp=(64 * q, 64 * q))
                                mm(psB[pq, x * 64:(x + 1) * 64], PTF[pq, x, :], ATT[pq, r, e * 64:(e + 1) * 64], tp=(64 * q, 64 * q))
                    cp("act", VPb[:], psA[:, 0:512].rearrange("p (x t) -> p x t", x=8))
                    cp("act", APb[:], psB[:, 0:512].rearrange("p (x t) -> p x t", x=8))
                    yield
                    for q in range(2):
                        pq = slice(64 * q, 64 * q + 64)
                        for r in range(4):
                            n = 2 * r + q
                            for e in range(2):
                                x = r * 2 + e
                                pe_ = slice(64 * e, 64 * e + 64)
                                fe = slice(e * 64, (e + 1) * 64)
                                ns = slice(n * 64, (n + 1) * 64)
                                tpo = (64 * q, 64 * e)
                                mm(psC[pe_, ns], APb[pq, x, :], BHT[pq, r, fe], tp=tpo)
                                mm(psA[pe_, 512 + n * 64:512 + (n + 1) * 64], BHT[pq, r, fe], VPb[pq, x, :], start=True, stop=False, tp=tpo)
                                mm(psA[pe_, 512 + n * 64:512 + (n + 1) * 64], KHT[pq, r, fe], VT[pq, r, fe], start=False, stop=True, tp=tpo)
                                mm(psB[pe_, 512 + n * 64:512 + (n + 1) * 64], APb[pq, x, :], ATRB[pq, x, :], tp=tpo)
                                mm(psA[pe_, ns], VPb[pq, x, :], ATRB[pq, x, :], start=True, stop=False, tp=tpo)
                                mm(psA[pe_, ns], VT[pq, r, fe], ATRK[pq, x, :], start=False, stop=True, tp=tpo)
                    cp("act", G0T[:], psC[:, :].rearrange("p (n j) -> p n j", n=8))
                    cp("dve", HS[:], psA[:, 512:1024].rearrange("p (n i) -> p n i", n=8))
                    tt("dve", RH[:], psB[:, 512:1024].rearrange("p (n t) -> p n t", n=8), AR[:, :, 64:128], ALU.add)
                    cp("act", Y0[:, :W], psA[:, 0:512])
                    yield

                def adv(g, k):
                    if g is None:
                        return
                    for _ in range(k):
                        try:
                            next(g)
                        except StopIteration:
                            return

                def back(ui, nxt):
                    c, tl = units[ui]
                    par = ppar[ui]
                    SGG, YB, HS, WC = SGG2[par], YB2[par], HS2[par], WC2[par]
                    G0T, RH, Y0 = G0T2[par], RH2[par], Y02[par]
                    W, gx, lc = tl["W"], tl["gx"], tl["lc"]
                    mark("u_dbl")
                    for n in range(8):
                        tok = slice(n * 64, (n + 1) * 64)
                        ns = slice(n * 64, (n + 1) * 64)
                        stt("dve", TMPS, ST[:, c, :], WC[:, n:n + 1], HS[:, n, :], ALU.mult, ALU.add)
                        for e in range(2):
                            pe_ = slice(64 * e, 64 * e + 64)
                            mm(psD[pe_, ns], G0T[pe_, n, :], STb[pe_, c, :], tp=(64 * e, 64 * e))
                            mm(psE[pe_, tok], STb[pe_, c, :], RH[pe_, n, :], tp=(64 * e, 64 * e))
                        tt("dve", STb[:, c, :], TMPS, psD[:, ns], ALU.add)
                        tt("dve", ST[:, c, :], TMPS, psD[:, ns], ALU.add)
                        adv(nxt, 2)
                    adv(nxt, 1000)
                    if tl["i"] == 3:
                        dma("sp", wkvp[l, c], ST[:, c, :])
                    mark("u_state")
                    tt("dve", YS[:, :W], psE[:, :W], Y0[:, :W], ALU.add)
                    cp("act", YBF[:, :W], YS[:, :W])
                    act(YSQ[:, :W], YS[:, :W], AF.Square)
                    mm(psD[:, 0:W], BMEAN, YBF[:, :W])
                    cp("act", T1[:, :W], psD[:, 0:W])
                    mm(psD[:, 0:W], BMEAN, YSQ[:, :W])
                    tt("pool", T2[:, :W], T1[:, :W], T1[:, :W], ALU.mult)
                    tt("dve", T2[:, :W], psD[:, 0:W], T2[:, :W], ALU.subtract)
                    act(T2[:, :W], T2[:, :W], AF.Sqrt, bias=EPSC[:, 1:2])
                    recip(T2[:, :W], T2[:, :W])
                    tt("pool", T1[:, :W], YS[:, :W], T1[:, :W], ALU.subtract)
                    tt("pool", T1[:, :W], T1[:, :W], T2[:, :W], ALU.mult)
                    act(T1[:, :W], T1[:, :W], AF.Identity, bias=V(l, O_LNB + c), scale=V(l, O_LNG + c))
                    tt("pool", T1[:, :W], T1[:, :W], YB[:, :W], ALU.add)
                    tt("pool", MIX[:, 2 + c, lc:lc + W], T1[:, :W], SGG[:, :W], ALU.mult)

                def chain(ui):
                    j = ui
                    while j < len(units):
                        yield from front(j)
                        if units[j][1]["kind"] == "P":
                            break
                        j += 1

                g0 = chain(0)
                adv(g0, 1000)
                for ui in range(len(units)):
                    if units[ui][1]["kind"] == "P":
                        back(ui, chain(ui + 1) if ui + 1 < len(units) else None)
                mark("rw_done")
                if any(t["kind"] == "P" and t["i"] == 3 for t in tls):
                    dma("sp", shiftp[l], PREV[:])
                if has_s:
                    stl = [t for t in tls if t["kind"] == "S"][0]
                    lc = stl["lc"]
                    dma("sp", shifts[l], SHS)
                    cf.o, cb.o = f_mark, b_mark
                    TM6 = cf.get(6 * 128).rearrange("p (v f) -> p v f", v=6)
                    for c in range(4):
                        for v in range(6):
                            tr(psA[0:16, v * 128:(v + 1) * 128], SV6[:, c, v, :], ID32)
                        cp("act", TM6[0:16], psA[0:16, 0:768].rearrange("p (v f) -> p v f", v=6))
                        for e in range(2):
                            dma("sp", scr6[:, 2 * c + e], TM6[0:16, :, e * 64:(e + 1) * 64])
                    V6 = cf.get(384).rearrange("p (v j) -> p v j", v=6)
                    BH3 = cf.get(192).rearrange("p (v j) -> p v j", v=3)
                    SS = cf.get(2048).rearrange("p (i j) -> p i j", i=32)
                    TP = cf.get(2048).rearrange("p (i j) -> p i j", i=32)
                    UU = cf.get(64); YY = cf.get(64); YC = cf.get(64); SC1 = cf.get(8); TMY = cf.get(512)
                    dma("sp", V6, scr6.rearrange("b h v j -> (b h) v j"))
                    dma("sp", BH3, bhv[l].rearrange("p (v j) -> p v j", v=3))
                    bj = lambda a: a.unsqueeze(1).to_broadcast([128, 32, 64])
                    bi = lambda a: a.unsqueeze(2).to_broadcast([128, 32, 64])
                    for ih in range(2):
                        hs = slice(ih * 32, ih * 32 + 32)
                        dma("sp", SS, swkv[l][:, ih * 2048:(ih + 1) * 2048].rearrange("p (i j) -> p i j", i=32))
                        tt("dve", TP, SS, bj(V6[:, 4, :]), ALU.mult)
                        rsum(UU[:, hs], TP)
                        tt("pool", SS, SS, bj(V6[:, 1, :]), ALU.mult)
                        tt("dve", TP, bi(UU[:, hs]), bj(V6[:, 5, :]), ALU.mult)
                        tt("pool", SS, SS, TP, ALU.subtract)
                        tt("dve", TP, bi(V6[:, 3, hs]), bj(V6[:, 2, :]), ALU.mult)
                        tt("pool", SS, SS, TP, ALU.add)
                        dma("sp", wkvs[l][:, ih * 2048:(ih + 1) * 2048].rearrange("p (i j) -> p i j", i=32), SS)
                        tt("dve", TP, SS, bj(V6[:, 0, :]), ALU.mult)
                        rsum(YY[:, hs], TP)
                    rsum(SC1[:, 0:1], YY)
                    ts("dve", SC1[:, 0:1], SC1[:, 0:1], 1.0 / 64, ALU.mult)
                    ts("dve", YC, YY, SC1[:, 0:1], ALU.subtract)
                    tt("dve", YY, YC, YC, ALU.mult)
                    rsum(SC1[:, 1:2], YY)
                    ts("dve", SC1[:, 1:2], SC1[:, 1:2], 1.0 / 64, ALU.mult)
                    act(SC1[:, 1:2], SC1[:, 1:2], AF.Sqrt, bias=EPSC[:, 1:2])
                    recip(SC1[:, 1:2], SC1[:, 1:2])
                    ts("dve", YC, YC, SC1[:, 1:2], ALU.mult)
                    tt("dve", YC, YC, BH3[:, 0, :], ALU.mult)
                    tt("dve", YC, YC, BH3[:, 1, :], ALU.add)
                    tt("dve", YY, V6[:, 0, :], V6[:, 2, :], ALU.mult)
                    tt("dve", YY, YY, BH3[:, 2, :], ALU.mult)
                    rsum(SC1[:, 2:3], YY)
                    stt("dve", YC, V6[:, 3, :], SC1[:, 2:3], YC, ALU.mult, ALU.add)
                    dma("sp", scry.rearrange("b (h i) -> (b h) i", h=8), YC)
                    dma("sp", TMY[0:16, :], scry)
                    for c in range(4):
                        tr(psC[:, c * 16:(c + 1) * 16], TMY[0:16, c * 128:(c + 1) * 128], ID32[0:16, 0:16])
                        tt("dve", MIX[:, 2 + c, lc:lc + NS], psC[:, c * 16:(c + 1) * 16], SGGS[:, c, :], ALU.mult)
                mark("rws_done")
                banks = [psA[:, 0:512], psA[:, 512:1024], psB[:, 0:512], psB[:, 512:1024], psD, psE]
                bk = 0
                nxt_ls = (l, si + 1) if si + 1 < len(SEGS) else ((l + 1, 0) if l + 1 < NL else None)
                ng = norm_gen(nxt_ls[0], tiles_of(SEGS[nxt_ls[1]])) if nxt_ls is not None else None
                for dc in range(8):
                    wo = WOB[dc % 2]
                    for tl in tls:
                        W, gx, lc = tl["W"], tl["gx"], tl["lc"]
                        pso = banks[bk % 6]
                        bk += 1
                        for mc in range(8):
                            mm(pso[:, :W], wo[:, mc, :], MIX[:, mc, lc:lc + W], start=(mc == 0), stop=(mc == 7))
                        tt("dve", X[:, dc, gx:gx + W], X[:, dc, gx:gx + W], pso[:, :W], ALU.add)
                        if ng is not None:
                            next(ng, None)
                    if dc + 2 < 8:
                        dma("pool", wo[:], wout[l, dc + 2])
                if ng is not None:
                    for _ in ng:
                        pass

        mark("out_done")
        cf.reset(); cb.reset()
        SQT = [cb.get(512), cb.get(512)]
        RSF = cf.get(512)
        YOB = [cf.get(512) for _ in range(4)]
        yi = 0
        for tl in tiles_of([("P", 0), ("P", 1), ("P", 2), ("P", 3), ("S", 0)]):
            W, gx = tl["W"], tl["gx"]
            rms_rstd(lambda kc: X[:, kc, gx:gx + W], W, RSF[:, :W], SQT, psC)
            for kc in range(8):
                yo = YOB[yi % 4]
                yi += 1
                stt(EW(), yo[:, :W], X[:, kc, gx:gx + W], FG[:, kc:kc + 1], RSF[:, :W], ALU.mult, ALU.mult)
                if tl["kind"] == "P":
                    dma("sp", yT[kc * 128:(kc + 1) * 128, gx:gx + W], yo[:, :W])
                else:
                    dma("sp", ysT[kc * 128:(kc + 1) * 128, :], yo[:, :W])
        import os as _os
        _stop = int(_os.environ.get("K_STOP", "0"))
        if _stop:
            del P.ops[_stop:]
            for k_ in P.lanes:
                P.lanes[k_] = [o for o in P.lanes[k_] if o.id < _stop]
            P.lanes = {k_: v_ for k_, v_ in P.lanes.items() if v_}
        P.add("sp", lambda e: None, R=[])
        fin = P.ops[-1]
        for lane, lst in P.lanes.items():
            fin.deps.add(lst[-1])

        sems = {}
        for nm in ["pe", "act", "dve", "pool"] + ["sp%d" % i for i in range(8)] + ["pool%d" % i for i in range(8)] + ["act%d" % i for i in range(4)]:
            sems[nm] = es.enter_context(nc.semaphore("s_" + nm))
        block = es.enter_context(nc.Block())
        P.emit(nc, block, sems)
    return nc, P


def _consts():
    cst = np.zeros((128, NCST), np.float32)
    p = np.arange(128)
    cst[:, 0:128] = np.eye(128, dtype=np.float32)
    t = np.arange(64)
    pm = (p % 64)[:, None]
    cst[:, 128:192] = (pm < t[None, :])
    cst[:, 192:256] = (pm <= t[None, :])
    cst[:, 256:320] = (t[None, :] < pm)
    cst[:, 320:576] = np.eye(16, dtype=np.float32).reshape(1, 256)
    wins = np.array([[2, 8], [4, 16]], np.float32)
    for ch in range(2):
        for half in range(2):
            w = wins[half, ch]
            cst[half * 64:(half + 1) * 64, 576 + ch] = 1.0 / w
            cst[half * 64:(half + 1) * 64, 578 + ch * 16: 578 + ch * 16 + 16] = 1.0 / np.minimum(np.arange(16) + 1, w)
    seg = np.ones(512, np.float32)
    seg[::64] = 0.0
    cst[:, 610:1122] = seg[None, :]
    cb = np.zeros((128, NCSTB), np.float32)
    cb[:, 0:128] = np.eye(128)
    cb[:, 128:256] = 1.0 / 1024
    blk = np.zeros((128, 128), np.float32)
    blk[0:64, 0:64] = 1
    blk[64:128, 64:128] = 1
    cb[:, 256:384] = blk
    cb[:, 384:512] = blk / 64
    cb[:, 512:576] = (pm == t[None, :])
    cb[:, 576:640] = 1.0
    return cst, cb


def _fm(v, n):
    return np.ascontiguousarray(v.reshape(n, 128).T)


def kernel(x_prompt, x_sample, mem_prompt, state_pool, state_shift, state_wkv, cache_mem_k, cache_mem_v,
           norm_g, w_in, w_out, pool_w, pool_scale, shift_mu, w0, w_w2, a0, w_a2, k_k, k_a, r_k,
           ln_x_g, ln_x_b, mem_norm_g, w_kv, final_norm_g, _NL=L_ALL, _trace=False):
    f = lambda a: np.asarray(a, np.float32)
    x_prompt, x_sample, mem_prompt = f(x_prompt), f(x_sample), f(mem_prompt)
    L = L_ALL
    cst, cstb = _consts()
    vec = np.zeros((128, VL * L + 8), np.float32)
    for l in range(L):
        b = l * VL
        vec[:, b + 0:b + 8] = _fm(f(norm_g)[l], 8)
        vec[:, b + 8:b + 21] = _fm(f(shift_mu)[l], 13)
        vec[:, b + 34:b + 38] = _fm(f(w0)[l], 4)
        vec[:, b + 38:b + 42] = _fm(f(a0)[l], 4)
        vec[:, b + 42:b + 46] = _fm(f(k_k)[l], 4)
        vec[:, b + 46:b + 50] = _fm(f(k_a)[l], 4)
        vec[:, b + 54:b + 58] = _fm(f(r_k)[l].reshape(-1), 4)
        vec[:, b + 58:b + 62] = _fm(f(ln_x_g)[l], 4)
        vec[:, b + 62:b + 66] = _fm(f(ln_x_b)[l], 4)
        vec[:, b + 66:b + 68] = _fm(f(pool_scale)[l], 2)
        vec[:, b + 68:b + 76] = _fm(f(mem_norm_g)[l], 8)
    vec[:, VL * L:VL * L + 8] = _fm(f(final_norm_g), 8)
    w_in = f(w_in)
    win = np.ascontiguousarray(w_in.reshape(L, 8, 128, 25, 128)[:, :, :, FC_ORDER, :].transpose(0, 3, 2, 1, 4))
    wout = np.ascontiguousarray(f(w_out).reshape(L, 8, 128, 8, 128).transpose(0, 3, 2, 1, 4))
    wkv = np.ascontiguousarray(f(w_kv).reshape(L, 8, 128, 512).transpose(0, 2, 1, 3))
    lora2 = np.ascontiguousarray(np.concatenate([f(w_w2), f(w_a2)], axis=1))
    pw = f(pool_w)
    pwb = np.zeros((L, 2, 128, 128), np.float32)
    for ch in range(2):
        pwb[:, ch, 0:64, 0:64] = pw[:, 2 * ch]
        pwb[:, ch, 64:128, 64:128] = pw[:, 2 * ch + 1]
    bh = np.stack([f(ln_x_g).reshape(L, 8, 64), f(ln_x_b).reshape(L, 8, 64), f(r_k).reshape(L, 8, 64)], axis=2)
    bhv = np.ascontiguousarray(np.broadcast_to(bh[:, None], (L, 16, 8, 3, 64)).reshape(L, 128, 192))
    shared = dict(win=win, wout=wout, wkv=wkv, vec=vec, lora2=lora2, pwb=pwb, cst=cst, cstb=cstb, bhv=bhv)
    sp, ss, sw, ck, cvv = f(state_pool), f(state_shift), f(state_wkv), f(cache_mem_k), f(cache_mem_v)
    in_maps = []
    for c in range(8):
        bs = slice(16 * c, 16 * c + 16)
        m = dict(shared)
        m["xT"] = np.ascontiguousarray(x_prompt[c].T)
        m["xsT"] = np.ascontiguousarray(x_sample[bs, 0, :].T)
        m["memT"] = np.ascontiguousarray(mem_prompt[c].T)
        m["spT"] = np.ascontiguousarray(sp[:, bs].transpose(0, 3, 1, 2))
        m["sshT"] = np.ascontiguousarray(ss[:, bs].transpose(0, 2, 1))
        m["swkv"] = np.ascontiguousarray(sw[:, bs].reshape(L, 128, 4096))
        m["ckT"] = np.ascontiguousarray(ck[:, bs].reshape(L, 16, 256, 2, 128).transpose(0, 1, 3, 4, 2))
        m["cv"] = np.ascontiguousarray(cvv[:, bs].reshape(L, 16, 256, 256))
        in_maps.append(m)
    nc, _ = build_program(_NL)
    res = run_bass_kernel_spmd(nc, in_maps, core_ids=list(range(8)), **({"trace": True} if _trace else {}))
    R = res.results
    y_prompt = np.stack([R[c]["yT"].T for c in range(8)])
    y_sample = np.concatenate([R[c]["ysT"].T for c in range(8)])[:, None, :]
    pool_prompt = np.stack([R[c]["poolp"].transpose(0, 2, 1) for c in range(8)], axis=1)
    shift_prompt = np.stack([R[c]["shiftp"].transpose(0, 2, 1).reshape(L, 1664) for c in range(8)], axis=1)
    wkv_prompt = np.stack([R[c]["wkvp"].reshape(L, 4, 2, 64, 64).transpose(0, 1, 2, 4, 3).reshape(L, 8, 64, 64) for c in range(8)], axis=1)
    memk_prompt = np.stack([R[c]["memk"].reshape(L, 256, 256).transpose(0, 2, 1).reshape(L, 256, 4, 64) for c in range(8)], axis=1)
    memv_prompt = np.stack([R[c]["memv"].reshape(L, 256, 4, 64) for c in range(8)], axis=1)
    pool_sample = np.concatenate([R[c]["pools"].transpose(0, 2, 3, 1) for c in range(8)], axis=1)
    shift_sample = np.concatenate([R[c]["shifts"].transpose(0, 3, 2, 1).reshape(L, 16, 1664) for c in range(8)], axis=1)
    wkv_sample = np.concatenate([R[c]["wkvs"].reshape(L, 16, 8, 64, 64) for c in range(8)], axis=1)
    outs = (y_prompt, y_sample, pool_prompt, shift_prompt, wkv_prompt, memk_prompt, memv_prompt, pool_sample, shift_sample, wkv_sample)
    outs = tuple(np.ascontiguousarray(o, dtype=np.float32) for o in outs)
    if _trace:
        return outs, res
    return outs
```
